# Optimizing a Trainium2 kernel written in Bass

```python
import math
import jax, jax.numpy as jnp
from jax import lax
import numpy as np


D_MODEL = 2048
BATCH = 4
SEQ = 8192
DEPTH = 2
DEC_BATCH = 8
DEC_SEQ = 16
PAST_LEN = 2048

CHUNK = 64
GROUP_WIDTH = D_MODEL // 4
MIX_WIDTH = 4 * GROUP_WIDTH
GMLP_CHUNK = 128
A_HEADS = 4
A_HEAD_DIM = GROUP_WIDTH // A_HEADS
B_CONV_WIDTH = 31
C_HEAD_DIM = 64
C_HEADS = GROUP_WIDTH // C_HEAD_DIM
C_GROUPS = 2
C_STATE = 128
C_CONV_WIDTH = 4
C_XBC = GROUP_WIDTH + 2 * C_GROUPS * C_STATE
SSD_BLOCK = CHUNK
D_CONV_WIDTH = 3
D_FF = 5632
IN_A = 2 * GROUP_WIDTH
IN_B = 2 * GROUP_WIDTH
IN_C = GROUP_WIDTH + C_XBC + C_HEADS
IN_D = 3 * GROUP_WIDTH
IN_WIDTH = IN_A + IN_B + IN_C + IN_D
NORM_EPS = 1e-6
F32 = jnp.float32

kernel_name = 'hybrid_parallel_group_streaming_encoder_step'


def rms_norm(x, g):
    xf = x.astype(F32)
    y = xf * lax.rsqrt(jnp.mean(xf * xf, axis=-1, keepdims=True) + NORM_EPS)
    return (y * g.astype(F32)).astype(x.dtype)


def layer_norm(x, g, b):
    xf = x.astype(F32)
    mu = jnp.mean(xf, axis=-1, keepdims=True)
    xc = xf - mu
    y = xc * lax.rsqrt(jnp.mean(xc * xc, axis=-1, keepdims=True) + NORM_EPS)
    return (y * g.astype(F32) + b.astype(F32)).astype(x.dtype)


def swiglu(h, w_in, w_out):
    g, u = jnp.split(h @ w_in, 2, axis=-1)
    return (jax.nn.silu(g) * u) @ w_out


def causal_dwconv(x, buf, w, b):
    K, C = w.shape
    xp = jnp.concatenate([buf.astype(x.dtype), x], axis=1)
    y = lax.conv_general_dilated(xp, w[:, None, :].astype(x.dtype), window_strides=(1,), padding='VALID',
                                 dimension_numbers=('NWC', 'WIO', 'NWC'), feature_group_count=C)
    if b is not None:
        y = y + b.astype(x.dtype)
    return y, xp[:, -(K - 1):]


def gmlp_mix(p, ln_g, ln_b, ws, bs):
    n, L, _ = p.shape
    u, v = jnp.split(jax.nn.gelu(p, approximate=False), 2, axis=-1)
    v = layer_norm(v, ln_g, ln_b)
    T = min(GMLP_CHUNK, L)
    vh = v.reshape(n, L // T, T, A_HEADS, A_HEAD_DIM)
    mask = jnp.tril(jnp.ones((T, T), bool))
    w = jnp.where(mask, ws[:, :T, :T], 0).astype(v.dtype)
    s = jnp.einsum('hts,bcshd->bcthd', w, vh) + bs[:, :T].T[None, None, :, :, None].astype(v.dtype)
    return u * s.reshape(n, L, GROUP_WIDTH), v


def conformer_conv(p, buf, w, b, ln_g, ln_b):
    a, g = jnp.split(p, 2, axis=-1)
    y, new_buf = causal_dwconv(a * jax.nn.sigmoid(g), buf, w, b)
    return jax.nn.silu(layer_norm(y, ln_g, ln_b)), new_buf


def ssd_scan(x, dt, A, Bm, Cm, h0, block):
    nb, L, H, P = x.shape
    G, K = Bm.shape[2], Bm.shape[3]
    R = H // G
    nc = L // block
    xc = x.reshape(nb, nc, block, G, R, P)
    dtc = dt.reshape(nb, nc, block, G, R)
    Bc = Bm.reshape(nb, nc, block, G, K)
    Cc = Cm.reshape(nb, nc, block, G, K)
    acum = jnp.cumsum(dtc * A.reshape(G, R), axis=2)
    xdt = xc * dtc[..., None]
    seg = acum[:, :, :, None] - acum[:, :, None, :]
    causal = jnp.tril(jnp.ones((block, block), bool))[:, :, None, None]
    decay = jnp.exp(jnp.where(causal, seg, -jnp.inf))
    cb = jnp.einsum('bctgk,bcsgk->bctsg', Cc, Bc)
    y_intra = jnp.einsum('bctsgr,bcsgrp->bctgrp', cb[..., None] * decay, xdt)
    last = acum[:, :, -1]
    to_end = jnp.exp(last[:, :, None] - acum)
    s_block = jnp.einsum('bcsgk,bcsgrp->bcgrpk', Bc, xdt * to_end[..., None])

    def step(h, inp):
        dec, s = inp
        return dec[..., None, None] * h + s, h

    h_last, h_prev = lax.scan(step, h0.reshape(nb, G, R, P, K),
                              (jnp.moveaxis(jnp.exp(last), 1, 0), jnp.moveaxis(s_block, 1, 0)))
    h_prev = jnp.moveaxis(h_prev, 0, 1)
    y_inter = jnp.einsum('bctgk,bcgrpk->bctgrp', Cc, h_prev) * jnp.exp(acum)[..., None]
    y = (y_intra + y_inter).reshape(nb, L, H, P)
    return y, h_last.reshape(nb, H, P, K)


def mamba2_mix(p, conv_buf, h0, conv_w, conv_b, dt_bias, a_log, d_skip, norm_g):
    n, L, _ = p.shape
    z, xbc, dt = jnp.split(p, [GROUP_WIDTH, GROUP_WIDTH + C_XBC], axis=-1)
    xbc, new_buf = causal_dwconv(xbc, conv_buf, conv_w, conv_b)
    xbc = jax.nn.silu(xbc).astype(F32)
    xs, bm, cm = jnp.split(xbc, [GROUP_WIDTH, GROUP_WIDTH + C_GROUPS * C_STATE], axis=-1)
    xs = xs.reshape(n, L, C_HEADS, C_HEAD_DIM)
    bm = bm.reshape(n, L, C_GROUPS, C_STATE)
    cm = cm.reshape(n, L, C_GROUPS, C_STATE)
    dt = jax.nn.softplus(dt.astype(F32) + dt_bias.astype(F32))
    A = -jnp.exp(a_log.astype(F32))
    y, h = ssd_scan(xs, dt, A, bm, cm, h0.astype(F32), min(SSD_BLOCK, L))
    y = y + d_skip.astype(F32)[:, None] * xs
    gsz = GROUP_WIDTH // C_GROUPS
    y = y.reshape(n, L, C_GROUPS, gsz) * jax.nn.silu(z.astype(F32)).reshape(n, L, C_GROUPS, gsz)
    y = y * lax.rsqrt(jnp.mean(y * y, axis=-1, keepdims=True) + NORM_EPS)
    y = y.reshape(n, L, GROUP_WIDTH) * norm_g.astype(F32)
    return y.astype(p.dtype), new_buf, h.astype(h0.dtype)


def short_conv_mix(p, buf, w):
    bg, cg, xx = jnp.split(p, 3, axis=-1)
    y, new_buf = causal_dwconv(cg * xx, buf, w, None)
    return bg * y, new_buf


def layer(x, st_b, st_c, st_ssm, st_d, lw):
    (norm_g, ffn_w_in, ffn_w_out, w_in, w_out, a_ln_g, a_ln_b, a_ws, a_bs,
     b_conv_w, b_conv_b, b_ln_g, b_ln_b, c_conv_w, c_conv_b, c_dt_bias, c_a_log, c_d, c_norm_g,
     d_conv_w) = lw
    x = x + 0.5 * rms_norm(swiglu(rms_norm(x, norm_g[0]), ffn_w_in[0], ffn_w_out[0]), norm_g[1])
    h = rms_norm(x, norm_g[2])
    pa, pb, pc, pd = jnp.split(h @ w_in, [IN_A, IN_A + IN_B, IN_A + IN_B + IN_C], axis=-1)
    ya, va = gmlp_mix(pa, a_ln_g, a_ln_b, a_ws, a_bs)
    yb, nb = conformer_conv(pb, st_b, b_conv_w, b_conv_b, b_ln_g, b_ln_b)
    yc, nc, nh = mamba2_mix(pc, st_c, st_ssm, c_conv_w, c_conv_b, c_dt_bias, c_a_log, c_d, c_norm_g)
    yd, nd = short_conv_mix(pd, st_d, d_conv_w)
    mix = jnp.concatenate([ya, yb, yc, yd], axis=-1) @ w_out
    x = x + rms_norm(mix, norm_g[3])
    x = x + 0.5 * rms_norm(swiglu(rms_norm(x, norm_g[4]), ffn_w_in[1], ffn_w_out[1]), norm_g[5])
    return x, (va, nb, nc, nh, nd)


def trunk(x, st_b, st_c, st_ssm, st_d, weights):
    new_states = []
    for l in range(DEPTH):
        lw = tuple(w[l] for w in weights)
        x, st = layer(x, st_b[l], st_c[l], st_ssm[l], st_d[l], lw)
        new_states.append(st)
    v, nb, nc, nh, nd = [jnp.stack(s) for s in zip(*new_states)]
    return x, v, nb, nc, nh, nd


def setup_inputs(seed: int = 0) -> dict:
    key = jax.random.key(seed)
    ks = jax.random.split(key, 26)

    def nrm(k, shape, scale=1.0):
        return jax.random.normal(k, shape, F32) * scale

    dt0 = jnp.exp(jax.random.uniform(ks[20], (DEPTH, C_HEADS), F32, math.log(1e-3), math.log(1e-1)))
    dt_bias = dt0 + jnp.log(-jnp.expm1(-dt0))
    return {
        'x_prompt': nrm(ks[0], (BATCH, SEQ, D_MODEL)),
        'x_sample': nrm(ks[1], (DEC_BATCH, DEC_SEQ, D_MODEL)),
        'cache_conv_b': nrm(ks[2], (DEPTH, DEC_BATCH, B_CONV_WIDTH - 1, GROUP_WIDTH)),
        'cache_conv_c': nrm(ks[3], (DEPTH, DEC_BATCH, C_CONV_WIDTH - 1, C_XBC)),
        'state_ssm': nrm(ks[4], (DEPTH, DEC_BATCH, C_HEADS, C_HEAD_DIM, C_STATE), 0.5),
        'cache_conv_d': nrm(ks[5], (DEPTH, DEC_BATCH, D_CONV_WIDTH - 1, GROUP_WIDTH)),
        'norm_g': 1.0 + nrm(ks[6], (DEPTH, 6, D_MODEL), 0.05),
        'ffn_w_in': nrm(ks[7], (DEPTH, 2, D_MODEL, 2 * D_FF), D_MODEL ** -0.5),
        'ffn_w_out': nrm(ks[8], (DEPTH, 2, D_FF, D_MODEL), D_FF ** -0.5),
        'w_in': nrm(ks[9], (DEPTH, D_MODEL, IN_WIDTH), D_MODEL ** -0.5),
        'w_out': nrm(ks[10], (DEPTH, MIX_WIDTH, D_MODEL), MIX_WIDTH ** -0.5),
        'a_ln_g': 1.0 + nrm(ks[11], (DEPTH, GROUP_WIDTH), 0.05),
        'a_ln_b': nrm(ks[12], (DEPTH, GROUP_WIDTH), 0.01),
        'a_ws': nrm(ks[13], (DEPTH, A_HEADS, GMLP_CHUNK, GMLP_CHUNK), GMLP_CHUNK ** -0.5),
        'a_bs': 1.0 + nrm(ks[14], (DEPTH, A_HEADS, GMLP_CHUNK), 0.1),
        'b_conv_w': nrm(ks[15], (DEPTH, B_CONV_WIDTH, GROUP_WIDTH), B_CONV_WIDTH ** -0.5),
        'b_conv_b': nrm(ks[16], (DEPTH, GROUP_WIDTH), 0.01),
        'b_ln_g': 1.0 + nrm(ks[17], (DEPTH, GROUP_WIDTH), 0.05),
        'b_ln_b': nrm(ks[18], (DEPTH, GROUP_WIDTH), 0.01),
        'c_conv_w': nrm(ks[19], (DEPTH, C_CONV_WIDTH, C_XBC), C_CONV_WIDTH ** -0.5),
        'c_conv_b': nrm(ks[21], (DEPTH, C_XBC), 0.01),
        'c_dt_bias': dt_bias,
        'c_a_log': jnp.log(jax.random.uniform(ks[22], (DEPTH, C_HEADS), F32, 1.0, 16.0)),
        'c_d': 1.0 + nrm(ks[23], (DEPTH, C_HEADS), 0.1),
        'c_norm_g': 1.0 + nrm(ks[24], (DEPTH, GROUP_WIDTH), 0.05),
        'd_conv_w': nrm(ks[25], (DEPTH, D_CONV_WIDTH, GROUP_WIDTH), D_CONV_WIDTH ** -0.5),
    }


def reference(x_prompt, x_sample, cache_conv_b, cache_conv_c, state_ssm, cache_conv_d,
              norm_g, ffn_w_in, ffn_w_out, w_in, w_out, a_ln_g, a_ln_b, a_ws, a_bs,
              b_conv_w, b_conv_b, b_ln_g, b_ln_b, c_conv_w, c_conv_b, c_dt_bias, c_a_log, c_d, c_norm_g,
              d_conv_w):
    weights = (norm_g, ffn_w_in, ffn_w_out, w_in, w_out, a_ln_g, a_ln_b, a_ws, a_bs,
               b_conv_w, b_conv_b, b_ln_g, b_ln_b, c_conv_w, c_conv_b, c_dt_bias, c_a_log, c_d, c_norm_g,
               d_conv_w)
    nb = x_prompt.shape[0]
    dt = x_prompt.dtype
    zb = jnp.zeros((DEPTH, nb, B_CONV_WIDTH - 1, GROUP_WIDTH), dt)
    zc = jnp.zeros((DEPTH, nb, C_CONV_WIDTH - 1, C_XBC), dt)
    zh = jnp.zeros((DEPTH, nb, C_HEADS, C_HEAD_DIM, C_STATE), dt)
    zd = jnp.zeros((DEPTH, nb, D_CONV_WIDTH - 1, GROUP_WIDTH), dt)
    y_prompt, _, p_conv_b, p_conv_c, p_ssm, p_conv_d = trunk(x_prompt, zb, zc, zh, zd, weights)
    y_sample, s_gmlp_v, s_conv_b, s_conv_c, s_ssm, s_conv_d = trunk(
        x_sample, cache_conv_b, cache_conv_c, state_ssm, cache_conv_d, weights)
    return (y_prompt, y_sample, p_conv_b, p_conv_c, p_ssm, p_conv_d,
            s_gmlp_v, s_conv_b, s_conv_c, s_ssm, s_conv_d)
```

```python
import contextlib
import numpy as np
import concourse.bass as bass
import concourse.mybir as mybir
from concourse.bass_utils import run_bass_kernel_spmd

F32 = mybir.dt.float32
BF16 = mybir.dt.bfloat16
AF = mybir.ActivationFunctionType
ALU = mybir.AluOpType

D = 2048
DFF = 5632
NCH = 16
NHC = 44
INW = 5128
TP = 512
EPS = 1e-6
NPP = 292
O_NG, O_BCW, O_BCB, O_BLG, O_BLB, O_CCW, O_CCB, O_CDSK, O_CNG, O_DCW = 0, 96, 220, 224, 228, 232, 264, 272, 276, 280
FAST_RSTD = False
BCW = 1040


class _Op:
    __slots__ = ("eng", "fn", "deps", "signal", "dma", "lane", "semval", "eidx", "order")


class Sched:
    ENGS = ("pe", "act", "dve", "pool", "sp")

    def __init__(self):
        self.ops = []
        self.per = {e: [] for e in self.ENGS}
        self.last_w = {}
        self.readers = {}
        self.lane_last = {}

    def add(self, eng, fn, reads=(), writes=(), dma=None):
        op = _Op()
        op.eng = eng; op.fn = fn; op.signal = False; op.dma = dma is not None
        op.lane = dma; op.semval = None; op.order = len(self.ops)
        deps = []
        lw = self.last_w; rd = self.readers
        for k in reads:
            w = lw.get(k)
            if w is not None:
                deps.append(w)
        for k in writes:
            w = lw.get(k)
            if w is not None:
                deps.append(w)
            r = rd.get(k)
            if r:
                deps.extend(r)
        if op.dma:
            p = self.lane_last.get(dma)
            if p is not None:
                deps.append(p)
            self.lane_last[dma] = op
        for k in reads:
            r = rd.get(k)
            if r is None:
                rd[k] = [op]
            else:
                r.append(op)
        for k in writes:
            lw[k] = op
            rd[k] = []
        op.eidx = len(self.per[eng])
        best = {}
        for d in deps:
            if d is op:
                continue
            if d.dma:
                key = ("L", d.lane)
                if key not in best or best[key].order < d.order:
                    best[key] = d
            else:
                if d.eng == eng:
                    if eng == "pe" or eng == "sp":
                        continue
                    if op.eidx - d.eidx > 6:
                        continue
                key = ("E", d.eng)
                if key not in best or best[key].eidx < d.eidx:
                    best[key] = d
        op.deps = list(best.values())
        self.per[eng].append(op)
        self.ops.append(op)
        return op

    def emit(self, nc, es):
        for op in self.ops:
            for d in op.deps:
                d.signal = True
        esem = {e: es.enter_context(nc.semaphore("s_" + e)) for e in self.ENGS}
        lanes = {}
        lane_cnt = {}
        for op in self.ops:
            if op.dma:
                if op.lane not in lanes:
                    lanes[op.lane] = es.enter_context(nc.semaphore("l_%d" % len(lanes)))
                    lane_cnt[op.lane] = 0
                lane_cnt[op.lane] += 16
                op.semval = lane_cnt[op.lane]
        for e in self.ENGS:
            c = 0
            for op in self.per[e]:
                if not op.dma and op.signal:
                    c += 1
                    op.semval = c
        engobj = {"pe": "tensor", "act": "scalar", "dve": "vector", "pool": "gpsimd", "sp": "sync"}
        block = es.enter_context(nc.Block())

        def mk(e):
            ops = self.per[e]

            def body(eng):
                seen = {}
                for op in ops:
                    for d in op.deps:
                        if d.dma:
                            s = lanes[d.lane]; key = ("L", d.lane)
                        else:
                            s = esem[d.eng]; key = ("E", d.eng)
                        if seen.get(key, 0) >= d.semval:
                            continue
                        seen[key] = d.semval
                        eng.wait_ge(s, d.semval)
                    ins = op.fn(eng)
                    if op.dma:
                        ins.then_inc(lanes[op.lane], 16)
                    elif op.signal:
                        ins.then_inc(esem[e], 1)
                if e == "sp":
                    for ln, cnt in lane_cnt.items():
                        eng.wait_ge(lanes[ln], cnt)
            return body

        for e in self.ENGS:
            if self.per[e]:
                getattr(block, engobj[e])(mk(e))


def build(NP):
    nc = bass.Bass("TRN2", target_bir_lowering=False)
    S = Sched()
    A = S.add

    def din(name, shape):
        return nc.dram_tensor(name, shape, F32, kind="ExternalInput").ap()

    def dout(name, shape):
        return nc.dram_tensor(name, shape, F32, kind="ExternalOutput").ap()

    NTOK = max(NP, 1) * TP
    xT_d = din("xT", [D, NTOK])
    xsT_d = din("xsT", [D, 16])
    cb_in = din("cb_in", [2, 512, 30]); cc_in = din("cc_in", [2, 1024, 3])
    cd_in = din("cd_in", [2, 512, 2]); h_in = din("h_in", [2, 128, 512])
    wfi_d = din("wfi", [2, 2, D, 2 * DFF]); wfo_d = din("wfo", [2, 2, DFF, D])
    wi_d = din("wi", [2, D, INW]); wo_d = din("wo", [2, D, D])
    pp_d = din("pp", [128, 2 * NPP]); bcr_d = din("bcr", [1, 2 * BCW])
    wsT_d = din("wsT", [128, 2 * 4 * 128]); bsr_d = din("bsr", [1, 1024])

    yT_d = dout("yT", [D, NTOK]); ysT_d = dout("ysT", [D, 16])
    o_cb = {"p": dout("p_cb", [2, 512, 30]), "s": dout("s_cb", [2, 512, 30])}
    o_cc = {"p": dout("p_cc", [2, 1024, 3]), "s": dout("s_cc", [2, 1024, 3])}
    o_cd = {"p": dout("p_cd", [2, 512, 2]), "s": dout("s_cd", [2, 512, 2])}
    o_h = {"p": dout("p_h", [2, 128, 512]), "s": dout("s_h", [2, 128, 512])}
    s_v = dout("s_v", [2, 16, 512])

    def dscr(name, shape):
        return nc.dram_tensor(name, shape, BF16, kind="Internal").ap()

    wfi_b = [[dscr("wfib%d%d" % (l, f), [D, 2 * DFF]) for f in range(2)] for l in range(2)]
    wfo_b = [[dscr("wfob%d%d" % (l, f), [DFF, D]) for f in range(2)] for l in range(2)]
    wi_b = [dscr("wib%d" % l, [D, INW]) for l in range(2)]
    wo_b = [dscr("wob%d" % l, [D, D]) for l in range(2)]

    es = contextlib.ExitStack()
    with es:
        def sb(name, shape, dt=F32):
            return es.enter_context(nc.sbuf_tensor(name, shape, dt))

        xT = sb("xT_s", [128, NCH, TP])
        hT = sb("hT_s", [128, NCH, TP], BF16)
        ar = sb("arena", [128, 30, 512])
        slots = [sb("slot%d" % i, [128, 8192], BF16) for i in range(3)]
        cbuf = sb("cbuf", [128, 4, 516])
        cbufB = sb("cbufB", [128, 4, 544])
        hzt = sb("hzt", [128, 1024], BF16)
        zst = sb("zst", [128, 512])
        rs = sb("rs", [128, 2, 512])
        identf = sb("identf", [128, 128]); identb = sb("identb", [128, 128], BF16)
        onesf = sb("onesf", [128, 128]); onesb = sb("onesb", [128, 128], BF16)
        tri = sb("tri", [128, 128]); maskneg = sb("maskneg", [128, 128])
        wsTb = sb("wsTb", [128, 2, 4, 128], BF16)
        bsh = sb("bsh", [1, 1024], BF16); bsl = sb("bsl", [1, 1024], BF16)
        bc = sb("bc", [128, 2 * BCW])
        pp = sb("pp_s", [128, 2 * NPP]); ngs = sb("ngs", [128, 2, 96])
        wdt = sb("wdt", [128, 2, 16, 8], BF16)
        st_b = sb("st_b", [128, 2, 4, 30]); st_c = sb("st_c", [128, 2, 8, 3])
        st_d = sb("st_d", [128, 2, 4, 2]); st_h = sb("st_h", [128, 2, 512])
        sm = sb("sm", [128, 192])
        bnst = sb("bnst", [128, 8])
        psb = [es.enter_context(nc.psum_tensor("ps%d" % i, [128, 512], F32)) for i in range(8)]

        def pg(i):
            return ar[:, i, :]

        def pgb(i):
            return ar[:, i, :].bitcast(BF16)

        def ak(i):
            return "a%d" % i

        def aks(a, b):
            return [ak(i) for i in range(a, b)]

        def ppc(l, off):
            return pp[:, l * NPP + off: l * NPP + off + 1]

        pcount = [0]

        ps_reserved = set()

        def nextps():
            while True:
                i = pcount[0] % 8
                pcount[0] += 1
                if i not in ps_reserved:
                    return i

        bg = {"gen": None}

        def bg_step(n):
            g = bg["gen"]
            if g is None:
                return
            for _ in range(n):
                try:
                    next(g)
                except StopIteration:
                    bg["gen"] = None
                    return

        def bg_flush():
            while bg["gen"] is not None:
                bg_step(8)

        scount = [0]

        def nextslot():
            i = scount[0] % 3
            scount[0] += 1
            return i

        A("dve", lambda e: e.memset(onesf[:], 1.0), writes=["onesf"])
        A("dve", lambda e: e.memset(onesb[:], 1.0), writes=["onesb"])
        A("pool", lambda e: e.memset(identf[:], 1.0), writes=["identf"])
        A("pool", lambda e: e.affine_select(out=identf[:], in_=identf[:], pattern=[[-1, 128]], compare_op=ALU.is_equal,
                                            fill=0.0, base=0, channel_multiplier=1), reads=["identf"], writes=["identf"])
        A("dve", lambda e: e.tensor_copy(out=identb[:], in_=identf[:]), reads=["identf"], writes=["identb"])
        A("pool", lambda e: e.memset(tri[:], 1.0), writes=["tri"])
        A("pool", lambda e: e.affine_select(out=tri[:], in_=tri[:], pattern=[[1, 128]], compare_op=ALU.is_ge,
                                            fill=0.0, base=0, channel_multiplier=-1), reads=["tri"], writes=["tri"])
        A("pool", lambda e: e.memset(maskneg[:], 0.0), writes=["maskneg"])
        A("pool", lambda e: e.affine_select(out=maskneg[:], in_=maskneg[:], pattern=[[1, 128]], compare_op=ALU.is_ge,
                                            fill=-30000.0, base=0, channel_multiplier=-1), reads=["maskneg"], writes=["maskneg"])
        A("dve", lambda e: e.memset(hzt[:, :], 0.0), writes=["hz"])
        A("sp", lambda e: e.dma_start(out=pp[:], in_=pp_d[:, :]), writes=["pp"], dma="c0")
        A("sp", lambda e: e.dma_start(out=bc[:], in_=bcr_d.partition_broadcast(128)), writes=["bc"], dma="c1")
        wst_f = ar[:, 0:2, :].rearrange("p a (h t) -> p a h t", h=4)
        A("sp", lambda e: e.dma_start(out=ar[:, 0:2, :], in_=wsT_d.rearrange("p (a f) -> p a f", a=2)), writes=aks(0, 2), dma="c2")
        A("pool", lambda e: e.affine_select(out=wst_f, in_=wst_f, pattern=[[0, 2], [0, 4], [1, 128]], compare_op=ALU.is_ge,
                                            fill=0.0, base=0, channel_multiplier=-1), reads=aks(0, 2), writes=aks(0, 2))
        A("dve", lambda e: e.tensor_copy(out=wsTb[:], in_=wst_f), reads=aks(0, 2), writes=["wsTb"])
        A("sp", lambda e: e.dma_start(out=ar[0:1, 2, :], in_=bsr_d[:, 0:512]), writes=[ak(2)], dma="c3")
        A("sp", lambda e: e.dma_start(out=ar[0:1, 3, :], in_=bsr_d[:, 512:1024]), writes=[ak(3)], dma="c3")
        for a in range(2):
            A("dve", lambda e, a=a: e.tensor_copy(out=bsh[0:1, a * 512:(a + 1) * 512], in_=ar[0:1, 2 + a, :]), reads=[ak(2 + a)], writes=["bsh"])
            A("dve", lambda e, a=a: e.tensor_copy(out=ar[0:1, 4 + a, :], in_=bsh[0:1, a * 512:(a + 1) * 512]), reads=["bsh"], writes=[ak(4 + a)])
            A("dve", lambda e, a=a: e.tensor_tensor(out=ar[0:1, 4 + a, :], in0=ar[0:1, 2 + a, :], in1=ar[0:1, 4 + a, :], op=ALU.subtract),
              reads=[ak(2 + a), ak(4 + a)], writes=[ak(4 + a)])
            A("dve", lambda e, a=a: e.tensor_copy(out=bsl[0:1, a * 512:(a + 1) * 512], in_=ar[0:1, 4 + a, :]), reads=[ak(4 + a)], writes=["bsl"])
        for l in range(2):
            o = l * BCW + 1032
            A("act", lambda e, o=o: e.activation(out=bc[:, o:o + 8], in_=bc[:, o:o + 8], func=AF.Exp), reads=["bc"], writes=["bc"])
            A("dve", lambda e, o=o: e.tensor_scalar(out=bc[:, o:o + 8], in0=bc[:, o:o + 8], scalar1=-1.0, scalar2=None, op0=ALU.mult),
              reads=["bc"], writes=["bc"])
        sD = float(np.sqrt(D))
        for l in range(2):
            for i in range(6):
                c = sD * (0.5 if i in (1, 5) else 1.0)
                A("dve", lambda e, l=l, i=i, c=c: e.tensor_scalar(out=ngs[:, l, i * 16:(i + 1) * 16], in0=pp[:, l * NPP + i * 16: l * NPP + (i + 1) * 16],
                                                                  scalar1=c, scalar2=None, op0=ALU.mult), reads=["pp"], writes=["ngs"])
        wdt_loaded = set()

        castlane = [0]

        def cast_rows(name, dst, src, rows, rstep):
            for i, r0 in enumerate(range(0, rows, rstep)):
                r1 = min(rows, r0 + rstep)
                k = "w:%s:%d" % (name, i)
                ln = "cast%d" % (castlane[0] % 4)
                castlane[0] += 1
                A("pool", lambda e, r0=r0, r1=r1: e.dma_start(out=dst[r0:r1, :], in_=src[r0:r1, :]), writes=[k], dma=ln)

        def cast_cols(name, dst, src, bounds):
            for (c0, c1) in bounds:
                k = "w:%s:%d" % (name, c0)
                ln = "cast%d" % (castlane[0] % 4)
                castlane[0] += 1
                A("pool", lambda e, c0=c0, c1=c1: e.dma_start(out=dst[:, c0:c1], in_=src[:, c0:c1]), writes=[k], dma=ln)

        WI_BOUNDS = [(0, 512), (512, 1024), (1024, 1536), (1536, 2048), (2048, 2560), (2560, 3072), (3072, 3584), (3584, 3592),
                     (3592, 4104), (4104, 4616), (4616, 5128)]
        FI_BOUNDS = [(g * 512, (g + 1) * 512) for g in range(22)]
        for l in range(2):
            cast_cols("fi%d0" % l, wfi_b[l][0], wfi_d[l, 0], FI_BOUNDS)
            cast_rows("fo%d0" % l, wfo_b[l][0], wfo_d[l, 0], DFF, 512)
            cast_cols("wi%d" % l, wi_b[l], wi_d[l], WI_BOUNDS)
            cast_rows("wo%d" % l, wo_b[l], wo_d[l], D, 512)
            cast_cols("fi%d1" % l, wfi_b[l][1], wfi_d[l, 1], FI_BOUNDS)
            cast_rows("fo%d1" % l, wfo_b[l][1], wfo_d[l, 1], DFF, 512)

        def rstd_from_ps(pi, T, dst, addc):
            if FAST_RSTD:
                A("act", lambda e: e.activation(out=dst, in_=psb[pi][:, 0:T], func=AF.Abs_reciprocal_sqrt, bias=addc), reads=["ps%d" % pi], writes=["rs"])
                return
            A("act", lambda e: e.activation(out=dst, in_=psb[pi][:, 0:T], func=AF.Sqrt, bias=float(addc)), reads=["ps%d" % pi], writes=["rs"])
            A("dve", lambda e: e.reciprocal(out=dst, in_=dst), reads=["rs"], writes=["rs"])

        def rmsnorm_to_h(l, gi, T):
            for q in range(4):
                A("act", lambda e, q=q: e.activation(out=hT[:, 4 * q:4 * q + 4, 0:T], in_=xT[:, 4 * q:4 * q + 4, 0:T], func=AF.Square),
                  reads=["x%d" % c for c in range(4 * q, 4 * q + 4)], writes=["h%d" % c for c in range(4 * q, 4 * q + 4)])
            pi = nextps()
            for c in range(NCH):
                A("pe", lambda e, c=c: e.matmul(psb[pi][:, 0:T], lhsT=onesb[:], rhs=hT[:, c, 0:T], start=(c == 0), stop=(c == NCH - 1)),
                  reads=["h%d" % c, "onesb"], writes=["ps%d" % pi])
            rstd_from_ps(pi, T, rs[:, 0, 0:T], EPS * D)
            for c in range(NCH):
                A("dve", lambda e, c=c: e.scalar_tensor_tensor(out=hT[:, c, 0:T], in0=xT[:, c, 0:T], scalar=ngs[:, l, gi * 16 + c: gi * 16 + c + 1],
                                                               in1=rs[:, 0, 0:T], op0=ALU.mult, op1=ALU.mult),
                  reads=["x%d" % c, "rs", "ngs"], writes=["h%d" % c])

        STAGE_PAGES = [0, 1, 2, 3, 4, 5, 6, 7, 12, 13, 14, 15, 16, 17, 18, 19]

        def proj_postnorm(l, gi, T, wsrc, wk, nK, rhs_ap, rhs_keys, yh0, tp0, s_order=None, stage=False):
            wv = wsrc.rearrange("(k p) n -> p k n", p=128)
            if s_order is None:
                s_order = list(range(nK // 4))
            kfirst = 4 * s_order[0]; klast = 4 * s_order[-1] + 3
            tmp0 = ar[:, tp0, 0:T]; tk0 = ak(tp0)
            for h in range(2):
                for s in s_order:
                    si = nextslot()
                    sl = slots[si][:, :].rearrange("p (j n) -> p j n", j=4)
                    A("sp", lambda e, sl=sl, s=s, h=h: e.dma_start(out=sl[:, :, 0:1024], in_=wv[:, 4 * s:4 * s + 4, h * 1024:(h + 1) * 1024]),
                      reads=["w:%s:%d" % (wk, s)], writes=["slot%d" % si], dma="slot%d" % si)
                    for jj in range(4):
                        kc = 4 * s + jj
                        for m in range(8):
                            A("pe", lambda e, sl=sl, jj=jj, m=m, kc=kc: e.matmul(psb[m][:, 0:T], lhsT=sl[:, jj, m * 128:(m + 1) * 128], rhs=rhs_ap(kc),
                                                                              start=(kc == kfirst), stop=(kc == klast)),
                              reads=["slot%d" % si] + rhs_keys(kc), writes=["ps%d" % m])
                if h == 0:
                    for m in range(8):
                        if m % 2 == 0:
                            A("act", lambda e, m=m: e.activation(out=ar[:, yh0 + m, 0:T], in_=psb[m][:, 0:T], func=AF.Copy),
                              reads=["ps%d" % m], writes=[ak(yh0 + m)])
                        else:
                            A("dve", lambda e, m=m: e.tensor_copy(out=ar[:, yh0 + m, 0:T], in_=psb[m][:, 0:T]),
                              reads=["ps%d" % m], writes=[ak(yh0 + m)])
                    for m in range(8):
                        A("act", lambda e, m=m: e.activation(out=hT[:, m, 0:T], in_=ar[:, yh0 + m, 0:T], func=AF.Square),
                          reads=[ak(yh0 + m)], writes=["h%d" % m])
            A("act", lambda e: e.activation(out=tmp0, in_=psb[7][:, 0:T], func=AF.Copy), reads=["ps7"], writes=[tk0])
            for m in range(7):
                A("act", lambda e, m=m: e.activation(out=hT[:, 8 + m, 0:T], in_=psb[m][:, 0:T], func=AF.Square),
                  reads=["ps%d" % m], writes=["h%d" % (8 + m)])
            A("act", lambda e: e.activation(out=hT[:, 15, 0:T], in_=tmp0, func=AF.Square), reads=[tk0], writes=["h15"])
            for c in range(NCH):
                A("pe", lambda e, c=c: e.matmul(psb[7][:, 0:T], lhsT=onesb[:], rhs=hT[:, c, 0:T], start=(c == 0), stop=(c == NCH - 1)),
                  reads=["h%d" % c, "onesb"], writes=["ps7"])
            rstd_from_ps(7, T, rs[:, 0, 0:T], EPS * D)
            for c in range(NCH):
                if c < 8:
                    src = ar[:, yh0 + c, 0:T]; sk = ak(yh0 + c)
                elif c < 15:
                    src = psb[c - 8][:, 0:T]; sk = "ps%d" % (c - 8)
                else:
                    src = tmp0; sk = tk0
                tt = tp0 + 1 + (c % 3)
                A("dve", lambda e, c=c, src=src, tt=tt: e.scalar_tensor_tensor(out=ar[:, tt, 0:T], in0=src, scalar=ngs[:, l, gi * 16 + c: gi * 16 + c + 1],
                                                                               in1=rs[:, 0, 0:T], op0=ALU.mult, op1=ALU.mult),
                  reads=[sk, "rs", "ngs"], writes=[ak(tt)])
                if stage:
                    sp_ = STAGE_PAGES[c]
                    A("dve", lambda e, c=c, tt=tt, sp_=sp_: e.tensor_tensor(out=ar[:, sp_, 0:T], in0=xT[:, c, 0:T], in1=ar[:, tt, 0:T], op=ALU.add),
                      reads=[ak(tt), "x%d" % c], writes=[ak(sp_)])
                else:
                    A("dve", lambda e, c=c, tt=tt: e.tensor_tensor(out=xT[:, c, 0:T], in0=xT[:, c, 0:T], in1=ar[:, tt, 0:T], op=ALU.add),
                      reads=[ak(tt), "x%d" % c], writes=["x%d" % c])

        def hid_ap(j, T):
            return pgb(j // 2)[:, (j % 2) * 512:(j % 2) * 512 + T]

        def ffn(l, f, T, stage=False):
            rmsnorm_to_h(l, 0 if f == 0 else 4, T)
            wv = wfi_b[l][f].rearrange("(k p) n -> p k n", p=128)
            wk = "fi%d%d" % (l, f)
            hkeys = ["h%d" % c for c in range(NCH)]
            def ffn_evac(j, pg_, pu_):
                tt = 22 + (j % 3)
                A("act", lambda e: e.activation(out=ar[:, tt, 0:T], in_=psb[pg_][:, 0:T], func=AF.Silu),
                  reads=["ps%d" % pg_], writes=[ak(tt)])
                A("dve", lambda e: e.tensor_tensor(out=hid_ap(j, T), in0=ar[:, tt, 0:T], in1=psb[pu_][:, 0:T], op=ALU.mult),
                  reads=["ps%d" % pu_, ak(tt)], writes=[ak(j // 2)])

            combos = []
            for g in range(2):
                si = nextslot()
                sl = slots[si][:, :].rearrange("p (k n) -> p k n", k=16)
                A("sp", lambda e, sl=sl, g=g: e.dma_start(out=sl, in_=wv[:, :, g * 512:(g + 1) * 512]), reads=["w:%s:%d" % (wk, g * 512)], writes=["slot%d" % si], dma="slot%d" % si)
                for jq in range(2):
                    for off in (jq * 256, jq * 256 + 128):
                        combos.append((si, sl, off, nextps()))
            for kc in range(NCH):
                for (si, sl, off, pi) in combos:
                    A("pe", lambda e, sl=sl, pi=pi, off=off, kc=kc: e.matmul(psb[pi][:, 0:T], lhsT=sl[:, kc, off:off + 128], rhs=hT[:, kc, 0:T],
                                                                          start=(kc == 0), stop=(kc == NCH - 1)),
                      reads=["slot%d" % si, "h%d" % kc], writes=["ps%d" % pi])
            for j in range(4):
                ffn_evac(j, combos[2 * j][3], combos[2 * j + 1][3])
            for g in range(2, NHC // 2):
                si = nextslot()
                sl = slots[si][:, :].rearrange("p (k n) -> p k n", k=16)
                A("sp", lambda e, sl=sl, g=g: e.dma_start(out=sl, in_=wv[:, :, g * 512:(g + 1) * 512]), reads=["w:%s:%d" % (wk, g * 512)], writes=["slot%d" % si], dma="slot%d" % si)
                for jq in range(2):
                    j = 2 * g + jq
                    pg_, pu_ = nextps(), nextps()
                    for (pi, off) in ((pg_, jq * 256), (pu_, jq * 256 + 128)):
                        for kc in range(NCH):
                            A("pe", lambda e, sl=sl, pi=pi, off=off, kc=kc: e.matmul(psb[pi][:, 0:T], lhsT=sl[:, kc, off:off + 128], rhs=hT[:, kc, 0:T],
                                                                                  start=(kc == 0), stop=(kc == NCH - 1)),
                              reads=["slot%d" % si, "h%d" % kc], writes=["ps%d" % pi])
                    tt = 22 + (j % 3)
                    A("act", lambda e, pg_=pg_, tt=tt: e.activation(out=ar[:, tt, 0:T], in_=psb[pg_][:, 0:T], func=AF.Silu),
                      reads=["ps%d" % pg_], writes=[ak(tt)])
                    A("dve", lambda e, pu_=pu_, tt=tt, j=j: e.tensor_tensor(out=hid_ap(j, T), in0=ar[:, tt, 0:T], in1=psb[pu_][:, 0:T], op=ALU.mult),
                      reads=["ps%d" % pu_, ak(tt)], writes=[ak(j // 2)])
            proj_postnorm(l, 1 if f == 0 else 5, T, wfo_b[l][f], "fo%d%d" % (l, f), NHC,
                          lambda kc: hid_ap(kc, T), lambda kc: [ak(kc // 2)], 22, 8, stage=stage)

        def mix_ap(c, T0, T1):
            return pgb(22 + c // 2)[:, (c % 2) * 512 + T0:(c % 2) * 512 + T1]

        def mk_(c):
            return ak(22 + c // 2)

        def load_wi_slot(l, col0, ncols=512):
            si = nextslot()
            sl = slots[si][:, :].rearrange("p (k n) -> p k n", k=16)
            wv = wi_b[l].rearrange("(k p) n -> p k n", p=128)
            A("sp", lambda e: e.dma_start(out=sl[:, :, 0:ncols], in_=wv[:, :, col0:col0 + ncols]), reads=["w:wi%d:%d" % (l, col0)], writes=["slot%d" % si], dma="slot%d" % si)
            return si, sl

        def proj_fm(si, sl, m, T):
            pi = nextps()
            for kc in range(NCH):
                A("pe", lambda e, kc=kc: e.matmul(psb[pi][:, 0:T], lhsT=sl[:, kc, m * 128:(m + 1) * 128], rhs=hT[:, kc, 0:T],
                                                  start=(kc == 0), stop=(kc == NCH - 1)),
                  reads=["slot%d" % si, "h%d" % kc], writes=["ps%d" % pi])
            bg_step(4)
            return pi

        def conv_taps(eng, cb, cbk, l, K, woff, nchk, T, acc_page0, boff=None, chunk0=0):
            for k in range(K):
                for m in range(nchk):
                    cm = chunk0 + m
                    accm = ar[:, acc_page0 + m, 0:T]
                    wk_ = ppc(l, woff + cm * K + k)
                    if k == 0 and boff is not None:
                        A(eng, lambda e, m=m, accm=accm, wk_=wk_, cm=cm: e.tensor_scalar(out=accm, in0=cb[:, m, 0:T], scalar1=wk_, scalar2=ppc(l, boff + cm),
                                                                                       op0=ALU.mult, op1=ALU.add),
                          reads=[cbk % m, "pp"], writes=[ak(acc_page0 + m)])
                    elif k == 0:
                        A(eng, lambda e, m=m, accm=accm, wk_=wk_: e.tensor_scalar(out=accm, in0=cb[:, m, 0:T], scalar1=wk_, scalar2=None, op0=ALU.mult),
                          reads=[cbk % m, "pp"], writes=[ak(acc_page0 + m)])
                    elif eng == "pool":
                        tp_ = 16 + (m % 2)
                        A(eng, lambda e, m=m, wk_=wk_, k=k, tp_=tp_: e.tensor_scalar(out=ar[:, tp_, 0:T], in0=cb[:, m, k:k + T], scalar1=wk_, scalar2=None, op0=ALU.mult),
                          reads=[cbk % m, "pp"], writes=[ak(tp_)])
                        A(eng, lambda e, accm=accm, tp_=tp_: e.tensor_tensor(out=accm, in0=accm, in1=ar[:, tp_, 0:T], op=ALU.add),
                          reads=[ak(tp_), ak(acc_page0 + m)], writes=[ak(acc_page0 + m)])
                    else:
                        A(eng, lambda e, m=m, accm=accm, wk_=wk_, k=k: e.scalar_tensor_tensor(out=accm, in0=cb[:, m, k:k + T], scalar=wk_, in1=accm,
                                                                                             op0=ALU.mult, op1=ALU.add),
                          reads=[cbk % m, "pp", ak(acc_page0 + m)], writes=[ak(acc_page0 + m)])

        def mixers(l, T, blocks, sample):
            if l not in wdt_loaded:
                wdt_loaded.add(l)
                A("sp", lambda e: e.dma_start(out=wdt[:, l, :, :], in_=wi_b[l].rearrange("(c p) n -> p c n", p=128)[:, :, 3584:3592]),
                  reads=["w:wi%d:3584" % l], writes=["wdt%d" % l], dma="c4")
            rmsnorm_to_h(l, 2, T)
            bo = l * BCW
            cbk = "cb%d"; cbBk = "cB%d"
            cb_all = [cbk % m for m in range(4)]; cbB_all = [cbBk % m for m in range(4)]
            HB = 30
            A("dve", lambda e: e.tensor_copy(out=cbufB[:, :, 0:HB], in_=st_b[:, l, :, :]), reads=["st_b%d" % l], writes=cbB_all)
            sia, sla = load_wi_slot(l, 1024)
            sig, slg = load_wi_slot(l, 1536)
            bbanks = [(nextps(), nextps()) for m in range(4)]
            for kc in range(NCH):
                for m in range(4):
                    for (si_, sl_, pi_) in ((sia, sla, bbanks[m][0]), (sig, slg, bbanks[m][1])):
                        A("pe", lambda e, kc=kc, m=m, sl_=sl_, pi_=pi_: e.matmul(psb[pi_][:, 0:T], lhsT=sl_[:, kc, m * 128:(m + 1) * 128], rhs=hT[:, kc, 0:T],
                                                                              start=(kc == 0), stop=(kc == NCH - 1)),
                          reads=["slot%d" % si_, "h%d" % kc], writes=["ps%d" % pi_])
            for m in range(4):
                pa, pg2 = bbanks[m]
                tt = 16 + m % 2
                A("act", lambda e, pg2=pg2, tt=tt: e.activation(out=ar[:, tt, 0:T], in_=psb[pg2][:, 0:T], func=AF.Sigmoid), reads=["ps%d" % pg2], writes=[ak(tt)])
                A("dve", lambda e, pa=pa, tt=tt, m=m: e.tensor_tensor(out=cbufB[:, m, HB:HB + T], in0=ar[:, tt, 0:T], in1=psb[pa][:, 0:T], op=ALU.mult),
                  reads=["ps%d" % pa, ak(tt)], writes=[cbBk % m])
            A("dve", lambda e: e.tensor_copy(out=st_b[:, l, :, :], in_=cbufB[:, :, T:T + HB]), reads=cbB_all, writes=["st_b%d" % l])
            ps_reserved.add(7)
            zb = zst[:, :].bitcast(BF16)
            tbufs = [(pgb(16)[:, 0:512], ak(16)), (pgb(16)[:, 512:1024], ak(16)), (pgb(17)[:, 0:512], ak(17)), (pgb(17)[:, 512:1024], ak(17)),
                     (zb[:, 0:512], "zst"), (zb[:, 512:1024], "zst")]

            def b_conv_gen():
                pend = None
                n = 0
                for m in range(4):
                    for k in range(31):
                        tb, tk = tbufs[n % 4]
                        n += 1
                        wk_ = ppc(l, O_BCW + m * 31 + k)
                        if k % 4 == 3:
                            A("dve", lambda e, m=m, k=k, tb=tb, wk_=wk_: e.tensor_scalar(out=tb[:, 0:T], in0=cbufB[:, m, k:k + T], scalar1=wk_, scalar2=None, op0=ALU.mult),
                              reads=[cbBk % m, "pp"], writes=[tk])
                        else:
                            A("act", lambda e, m=m, k=k, tb=tb, wk_=wk_: e.activation(out=tb[:, 0:T], in_=cbufB[:, m, k:k + T], func=AF.Copy, scale=wk_),
                              reads=[cbBk % m, "pp"], writes=[tk])
                        if pend is not None:
                            pend()
                        def mm(m=m, k=k, tb=tb, tk=tk):
                            A("pe", lambda e: e.matmul(psb[7][:, 0:T], lhsT=identb[:], rhs=tb[:, 0:T], start=(k == 0), stop=(k == 30)),
                              reads=[tk, "identb"], writes=["ps7"])
                            if k == 30:
                                A("dve", lambda e: e.tensor_scalar(out=ar[:, 18 + m, 0:T], in0=psb[7][:, 0:T], scalar1=ppc(l, O_BCB + m), scalar2=None, op0=ALU.add),
                                  reads=["ps7", "pp"], writes=[ak(18 + m)])
                        pend = mm
                        yield
                pend()
                yield
            bg["gen"] = b_conv_gen()
            si, sl = load_wi_slot(l, 0)
            for m in range(4):
                pi = proj_fm(si, sl, m, T)
                A("act", lambda e, pi=pi, m=m: e.activation(out=ar[:, m, 0:T], in_=psb[pi][:, 0:T], func=AF.Gelu), reads=["ps%d" % pi], writes=[ak(m)])
            si, sl = load_wi_slot(l, 512)

            def a_vproj(b, c0, L):
                pi = nextps()
                for kc in range(NCH):
                    A("pe", lambda e, kc=kc, pi=pi: e.matmul(psb[pi][0:L, :], lhsT=hT[:, kc, c0:c0 + L], rhs=sl[:, kc, 0:512],
                                                             start=(kc == 0), stop=(kc == NCH - 1)),
                      reads=["slot%d" % si, "h%d" % kc], writes=["ps%d" % pi])
                vp = 4 + b % 2
                A("act", lambda e: e.activation(out=ar[0:L, vp, :], in_=psb[pi][0:L, :], func=AF.Gelu), reads=["ps%d" % pi], writes=[ak(vp)])
                A("dve", lambda e: e.bn_stats(out=bnst[0:L, 0:6], in_=ar[0:L, vp, :]), reads=[ak(vp)], writes=["bnst"])
                A("dve", lambda e: e.bn_aggr(out=bnst[0:L, 6:8], in_=bnst[0:L, 0:6]), reads=["bnst"], writes=["bnst"])
                A("dve", lambda e: e.tensor_scalar(out=bnst[0:L, 7:8], in0=bnst[0:L, 7:8], scalar1=EPS, scalar2=None, op0=ALU.add), reads=["bnst"], writes=["bnst"])
                A("act", lambda e: e.activation(out=bnst[0:L, 7:8], in_=bnst[0:L, 7:8], func=AF.Sqrt), reads=["bnst"], writes=["bnst"])
                A("dve", lambda e: e.reciprocal(out=bnst[0:L, 7:8], in_=bnst[0:L, 7:8]), reads=["bnst"], writes=["bnst"])
                A("dve", lambda e: e.tensor_scalar(out=ar[0:L, vp, :], in0=ar[0:L, vp, :], scalar1=bnst[0:L, 6:7], scalar2=bnst[0:L, 7:8],
                                                   op0=ALU.subtract, op1=ALU.mult), reads=["bnst", ak(vp)], writes=[ak(vp)])
                A("dve", lambda e: e.tensor_tensor(out=ar[0:L, vp, :], in0=ar[0:L, vp, :], in1=bc[0:L, bo:bo + 512], op=ALU.mult),
                  reads=["bc", ak(vp)], writes=[ak(vp)])
                A("dve", lambda e: e.tensor_tensor(out=ar[0:L, vp, :], in0=ar[0:L, vp, :], in1=bc[0:L, bo + 512:bo + 1024], op=ALU.add),
                  reads=["bc", ak(vp)], writes=[ak(vp)])
                if sample:
                    A("pool", lambda e: e.dma_start(out=s_v[l, :, :], in_=ar[0:L, vp, :]), reads=[ak(vp)], dma="o_v")
                vb = pgb(6)[:, (b % 2) * 512:(b % 2) * 512 + 512]
                A("act", lambda e: e.activation(out=vb[0:L, :], in_=ar[0:L, vp, :], func=AF.Copy), reads=[ak(vp)], writes=["vb%d" % (b % 2)])

            def a_spatial(b, c0, L):
                vb = pgb(6)[:, (b % 2) * 512:(b % 2) * 512 + 512]
                vbk = "vb%d" % (b % 2)
                pi = nextps()
                for h in range(4):
                    o_ = psb[pi][:, h * 128:h * 128 + L]
                    A("pe", lambda e, o_=o_, h=h: e.matmul(o_, lhsT=vb[0:L, h * 128:(h + 1) * 128], rhs=wsTb[0:L, l, h, 0:L], start=True, stop=False),
                      reads=[vbk, "wsTb"], writes=["ps%d" % pi])
                    A("pe", lambda e, o_=o_, h=h: e.matmul(o_, lhsT=onesb[0:1, :], rhs=bsh[0:1, l * 512 + h * 128: l * 512 + h * 128 + L], start=False, stop=False),
                      reads=["bsh", "onesb"], writes=["ps%d" % pi])
                    A("pe", lambda e, o_=o_, h=h: e.matmul(o_, lhsT=onesb[0:1, :], rhs=bsl[0:1, l * 512 + h * 128: l * 512 + h * 128 + L], start=False, stop=True),
                      reads=["bsl", "onesb"], writes=["ps%d" % pi])
                for h in range(4):
                    A("dve", lambda e, h=h: e.tensor_tensor(out=mix_ap(h, c0, c0 + L), in0=ar[:, h, c0:c0 + L], in1=psb[pi][:, h * 128:h * 128 + L], op=ALU.mult),
                      reads=["ps%d" % pi, ak(h)], writes=[mk_(h)])

            for b, (c0, L) in enumerate(blocks):
                a_vproj(b, c0, L)
                if b > 0:
                    a_spatial(b - 1, *blocks[b - 1])
            a_spatial(len(blocks) - 1, *blocks[-1])
            HD = 2
            A("dve", lambda e: e.tensor_copy(out=cbuf[:, :, 0:HD], in_=st_d[:, l, :, :]), reads=["st_d%d" % l], writes=cb_all)
            sic, slc = load_wi_slot(l, 4104)
            six, slx = load_wi_slot(l, 4616)
            for m in range(4):
                pc = proj_fm(sic, slc, m, T)
                px = proj_fm(six, slx, m, T)
                tt = 8 + m % 2
                A("dve", lambda e, pc=pc, tt=tt: e.tensor_copy(out=ar[:, tt, 0:T], in_=psb[pc][:, 0:T]), reads=["ps%d" % pc], writes=[ak(tt)])
                A("dve", lambda e, px=px, tt=tt, m=m: e.tensor_tensor(out=cbuf[:, m, HD:HD + T], in0=ar[:, tt, 0:T], in1=psb[px][:, 0:T], op=ALU.mult),
                  reads=["ps%d" % px, ak(tt)], writes=[cbk % m])
            conv_taps("dve", cbuf, cbk, l, 3, O_DCW, 4, T, 10)
            A("dve", lambda e: e.tensor_copy(out=st_d[:, l, :, :], in_=cbuf[:, :, T:T + HD]), reads=cb_all, writes=["st_d%d" % l])
            sib, slb = load_wi_slot(l, 3592)
            for m in range(4):
                pi = proj_fm(sib, slb, m, T)
                A("dve", lambda e, pi=pi, m=m: e.tensor_tensor(out=mix_ap(12 + m, 0, T), in0=ar[:, 10 + m, 0:T], in1=psb[pi][:, 0:T], op=ALU.mult),
                  reads=[ak(10 + m), "ps%d" % pi], writes=[mk_(12 + m)])
            HC = 3
            for part in range(2):
                sip, slp = load_wi_slot(l, 2560 + part * 512)
                A("dve", lambda e, part=part: e.tensor_copy(out=cbuf[:, :, 0:HC], in_=st_c[:, l, 4 * part:4 * part + 4, :]), reads=["st_c%d" % l], writes=cb_all)
                for m in range(4):
                    pi = proj_fm(sip, slp, m, T)
                    A("dve", lambda e, pi=pi, m=m: e.tensor_copy(out=cbuf[:, m, HC:HC + T], in_=psb[pi][:, 0:T]), reads=["ps%d" % pi], writes=[cbk % m])
                conv_taps("dve", cbuf, cbk, l, 4, O_CCW, 4, T, 6, boff=O_CCB, chunk0=4 * part)
                A("dve", lambda e, part=part: e.tensor_copy(out=st_c[:, l, 4 * part:4 * part + 4, :], in_=cbuf[:, :, T:T + HC]), reads=cb_all, writes=["st_c%d" % l])
                for m in range(4):
                    if part == 0:
                        A("act", lambda e, m=m: e.activation(out=ar[:, m, 0:T], in_=ar[:, 6 + m, 0:T], func=AF.Silu), reads=[ak(6 + m)], writes=[ak(m)])
                    else:
                        A("act", lambda e, m=m: e.activation(out=pgb(4 + m // 2)[:, (m % 2) * 512:(m % 2) * 512 + T], in_=ar[:, 6 + m, 0:T], func=AF.Silu),
                          reads=[ak(6 + m)], writes=[ak(4 + m // 2)])

            def bct(m, c0, L):
                return pgb(4 + m // 2)[:, (m % 2) * 512 + c0:(m % 2) * 512 + c0 + L]

            xdtz = pgb(14).rearrange("p (h n) -> p h n", h=8)
            hz = hzt[:, :].rearrange("p (h n) -> p h n", h=8)
            A("dve", lambda e: e.memset(pgb(14), 0.0), writes=[ak(14)])
            H4 = st_h[:, l, :].rearrange("p (c two q) -> p c two q", c=4, two=2)
            hz4 = hzt[:, :].rearrange("p (c two n) -> p c two n", c=4, two=2)

            def refresh_hz():
                A("act", lambda e: e.activation(out=hz4[:, :, 0, 0:64], in_=H4[:, :, 0, :], func=AF.Copy), reads=["st_h%d" % l], writes=["hz"])
                A("act", lambda e: e.activation(out=hz4[:, :, 1, 64:128], in_=H4[:, :, 1, :], func=AF.Copy), reads=["st_h%d" % l], writes=["hz"])
            refresh_hz()
            Abc = bc[:, bo + 1032:bo + 1040]; dtb = bc[:, bo + 1024:bo + 1032]
            NB = len(blocks); L0 = blocks[0][1]; W8 = NB * 8
            smv = sm[:, :].rearrange("p (i n) -> p i n", i=6)
            dtv_all = smv[:, 0, 0:W8]; dtA_all = smv[:, 1, 0:W8]; acum_all = smv[:, 2, 0:W8]
            pdt = nextps()
            for b, (c0, L) in enumerate(blocks):
                for kc in range(NCH):
                    A("pe", lambda e, kc=kc, b=b, c0=c0, L=L: e.matmul(psb[pdt][0:L, b * 8:(b + 1) * 8], lhsT=hT[:, kc, c0:c0 + L], rhs=wdt[:, l, kc, :], start=(kc == 0), stop=(kc == NCH - 1)),
                      reads=["wdt%d" % l, "h%d" % kc], writes=["ps%d" % pdt])
            A("dve", lambda e: e.tensor_tensor(out=dtv_all[0:L0, :].rearrange("p (b n) -> p b n", b=NB), in0=psb[pdt][0:L0, 0:W8].rearrange("p (b n) -> p b n", b=NB),
                                               in1=dtb[0:L0, :].unsqueeze(1).broadcast_to([L0, NB, 8]), op=ALU.add), reads=["ps%d" % pdt, "bc"], writes=["smA"])
            A("act", lambda e: e.activation(out=dtv_all[0:L0, :], in_=dtv_all[0:L0, :], func=AF.Exp), reads=["smA"], writes=["smA"])
            A("act", lambda e: e.activation(out=dtv_all[0:L0, :], in_=dtv_all[0:L0, :], func=AF.Ln, bias=1.0), reads=["smA"], writes=["smA"])
            A("dve", lambda e: e.tensor_tensor(out=dtA_all[0:L0, :].rearrange("p (b n) -> p b n", b=NB), in0=dtv_all[0:L0, :].rearrange("p (b n) -> p b n", b=NB),
                                               in1=Abc[0:L0, :].unsqueeze(1).broadcast_to([L0, NB, 8]), op=ALU.mult), reads=["smA", "bc"], writes=["smA"])
            pcs = nextps()
            A("pe", lambda e: e.matmul(psb[pcs][0:L0, 0:W8], lhsT=tri[0:L0, 0:L0], rhs=dtA_all[0:L0, :], start=True, stop=True), reads=["smA", "tri"], writes=["ps%d" % pcs])
            A("dve", lambda e: e.tensor_copy(out=acum_all[0:L0, :], in_=psb[pcs][0:L0, 0:W8]), reads=["ps%d" % pcs], writes=["smA"])

            def ssd_block(b, c0, L):
                dtv = smv[:, 0, b * 8:(b + 1) * 8]; dtA = smv[:, 1, b * 8:(b + 1) * 8]; acum = smv[:, 2, b * 8:(b + 1) * 8]
                wdec = smv[:, 3, b * 8:(b + 1) * 8]; dtw = smv[:, 4, b * 8:(b + 1) * 8]; edec = smv[:, 5, b * 8:(b + 1) * 8]
                smk = "smb%d" % b
                rep = ar[:, 10:12, :].rearrange("p a (h m) -> p (a h) m", h=4)
                A("dve", lambda e, dtA=dtA: e.tensor_copy(out=rep[0:L, :, :], in_=dtA[0:L, :].unsqueeze(2).broadcast_to([L, 8, 128])), reads=["smA"], writes=aks(10, 12))
                pb0, pb1 = nextps(), nextps()
                for h in range(8):
                    pbi = pb0 if h < 4 else pb1
                    A("pe", lambda e, pbi=pbi, h=h: e.matmul(psb[pbi][:, (h % 4) * 128:(h % 4) * 128 + L], lhsT=rep[0:L, h, :], rhs=tri[0:L, 0:L], start=True, stop=True),
                      reads=aks(10, 12) + ["tri"], writes=["ps%d" % pbi])

                def bcv(pbi, L=L):
                    return psb[pbi][:, :].rearrange("p (h t) -> p h t", h=4)[:, :, 0:L]
                Cs = pgb(13).rearrange("p (h t) -> p h t", h=8)
                for half, pbi in enumerate((pb0, pb1)):
                    A("act", lambda e, half=half, pbi=pbi: e.activation(out=Cs[:, 4 * half:4 * half + 4, 0:L], in_=bcv(pbi), func=AF.Exp),
                      reads=["ps%d" % pbi], writes=[ak(13)])
                    A("act", lambda e, half=half, pbi=pbi, edec=edec: e.activation(out=edec[:, 4 * half:4 * half + 4], in_=bcv(pbi)[:, :, L - 1], func=AF.Exp),
                      reads=["ps%d" % pbi], writes=[smk])
                    A("dve", lambda e, half=half, pbi=pbi, wdec=wdec, acum=acum: e.tensor_tensor(out=wdec[0:L, 4 * half:4 * half + 4], in0=bcv(pbi)[0:L, :, L - 1], in1=acum[0:L, 4 * half:4 * half + 4], op=ALU.subtract),
                      reads=["ps%d" % pbi, smk, "smA"], writes=[smk])
                A("act", lambda e, wdec=wdec: e.activation(out=wdec[0:L, :], in_=wdec[0:L, :], func=AF.Exp), reads=[smk], writes=[smk])
                A("dve", lambda e, wdec=wdec, dtw=dtw, dtv=dtv: e.tensor_tensor(out=dtw[0:L, :], in0=wdec[0:L, :], in1=dtv[0:L, :], op=ALU.mult), reads=[smk, "smA"], writes=[smk])
                d1 = ar[:, 10:12, :].rearrange("p a (h t) -> p (a h) t", h=4)
                for half, pbi in enumerate((pb0, pb1)):
                    A("dve", lambda e, half=half, pbi=pbi: e.tensor_tensor(out=d1[0:L, 4 * half:4 * half + 4, 0:L], in0=bcv(pbi)[0:L], in1=maskneg[0:L, 0:L].unsqueeze(1).broadcast_to([L, 4, L]), op=ALU.add),
                      reads=["ps%d" % pbi, "maskneg"], writes=aks(10, 12))
                A("dve", lambda e, acum=acum: e.tensor_tensor(out=d1[0:L, :, 0:L], in0=d1[0:L, :, 0:L], in1=acum[0:L, :].unsqueeze(2).broadcast_to([L, 8, L]), op=ALU.subtract),
                  reads=aks(10, 12) + ["smA"], writes=aks(10, 12))
                Mb = pgb(12).rearrange("p (h t) -> p h t", h=8)
                A("act", lambda e: e.activation(out=Mb[0:L, :, 0:L], in_=d1[0:L, :, 0:L], func=AF.Exp), reads=aks(10, 12), writes=[ak(12)])
                pG = nextps()
                for g in range(2):
                    A("pe", lambda e, g=g, pG=pG: e.matmul(psb[pG][0:L, g * 128:g * 128 + L], lhsT=bct(g, c0, L), rhs=bct(2 + g, c0, L), start=True, stop=True),
                      reads=aks(4, 6), writes=["ps%d" % pG])
                for g in range(2):
                    A("dve", lambda e, g=g, pG=pG: e.tensor_tensor(out=Mb[0:L, 4 * g:4 * g + 4, 0:L], in0=Mb[0:L, 4 * g:4 * g + 4, 0:L],
                                                                   in1=psb[pG][0:L, g * 128:g * 128 + L].unsqueeze(1).broadcast_to([L, 4, L]), op=ALU.mult),
                      reads=[ak(12), "ps%d" % pG], writes=[ak(12)])
                for g in range(2):
                    A("dve", lambda e, g=g: e.tensor_tensor(out=Cs[:, 4 * g:4 * g + 4, 0:L], in0=Cs[:, 4 * g:4 * g + 4, 0:L],
                                                            in1=bct(2 + g, c0, L).unsqueeze(1).broadcast_to([128, 4, L]), op=ALU.mult),
                      reads=[ak(13)] + aks(4, 6), writes=[ak(13)])
                px = nextps()
                for m in range(4):
                    A("pe", lambda e, m=m, px=px: e.transpose(psb[px][0:L, m * 128:(m + 1) * 128], ar[:, m, c0:c0 + L], identf[:]),
                      reads=[ak(m), "identf"], writes=["ps%d" % px])
                pxv = psb[px][:, :].rearrange("p (c two q) -> p c two q", c=4, two=2)
                xz4 = pgb(14).rearrange("p (c two n) -> p c two n", c=4, two=2)
                dt4 = dtv.rearrange("p (c two) -> p c two", two=2)
                for two in range(2):
                    A("dve", lambda e, two=two, pxv=pxv, xz4=xz4, dt4=dt4: e.tensor_tensor(out=xz4[0:L, :, two, two * 64:two * 64 + 64], in0=pxv[0:L, :, two, :],
                                                                                     in1=dt4[0:L, :, two].unsqueeze(2).broadcast_to([L, 4, 64]), op=ALU.mult),
                      reads=["ps%d" % px, "smA"], writes=[ak(14)])
                xdte = pgb(15)[:, 0:512]
                A("dve", lambda e, px=px, xdte=xdte, dtw=dtw: e.tensor_tensor(out=xdte[0:L, :].rearrange("p (h q) -> p h q", h=8), in0=psb[px][0:L, :].rearrange("p (h q) -> p h q", h=8),
                                                                     in1=dtw[0:L, :].unsqueeze(2).broadcast_to([L, 8, 64]), op=ALU.mult),
                  reads=["ps%d" % px, smk], writes=[ak(15)])
                pB = nextps()
                pBb = psb[pB][:, :].bitcast(BF16)
                for g in range(2):
                    A("pe", lambda e, g=g, pBb=pBb: e.transpose(pBb[0:L, g * 128:(g + 1) * 128], bct(g, c0, L), identb[:]),
                      reads=aks(4, 6) + ["identb"], writes=["ps%d" % pB])
                Btm = pgb(15)[:, 512:768]
                A("act", lambda e, pBb=pBb, Btm=Btm: e.activation(out=Btm[0:L, :], in_=pBb[0:L, 0:256], func=AF.Copy), reads=["ps%d" % pB], writes=[ak(15)])
                py = nextps()
                for c in range(4):
                    o_ = psb[py][:, c * 128:c * 128 + L]
                    for two in range(2):
                        h = 2 * c + two
                        A("pe", lambda e, o_=o_, h=h, two=two: e.matmul(o_, lhsT=xdtz[0:L, h, :], rhs=Mb[0:L, h, 0:L], start=(two == 0), stop=False),
                          reads=[ak(14), ak(12)], writes=["ps%d" % py])
                    for two in range(2):
                        h = 2 * c + two
                        A("pe", lambda e, o_=o_, h=h, two=two: e.matmul(o_, lhsT=hz[:, h, :], rhs=Cs[:, h, 0:L], start=False, stop=(two == 1)),
                          reads=["hz", ak(13)], writes=["ps%d" % py])
                for c in range(4):
                    A("dve", lambda e, c=c, py=py: e.scalar_tensor_tensor(out=ar[:, 6 + c, c0:c0 + L], in0=ar[:, c, c0:c0 + L], scalar=ppc(l, O_CDSK + c),
                                                                          in1=psb[py][:, c * 128:c * 128 + L], op0=ALU.mult, op1=ALU.add),
                      reads=[ak(c), "pp", "ps%d" % py], writes=[ak(6 + c)])
                pS = nextps()
                for g in range(2):
                    A("pe", lambda e, g=g, pS=pS, Btm=Btm, xdte=xdte: e.matmul(psb[pS][:, g * 256:(g + 1) * 256], lhsT=Btm[0:L, g * 128:(g + 1) * 128], rhs=xdte[0:L, g * 256:(g + 1) * 256], start=True, stop=True),
                      reads=[ak(15)], writes=["ps%d" % pS])
                Hv = st_h[:, l, :].rearrange("p (h q) -> p h q", h=8)
                A("dve", lambda e, Hv=Hv, edec=edec: e.tensor_tensor(out=Hv, in0=Hv, in1=edec.unsqueeze(2).broadcast_to([128, 8, 64]), op=ALU.mult), reads=[smk, "st_h%d" % l], writes=["st_h%d" % l])
                A("dve", lambda e, pS=pS: e.tensor_tensor(out=st_h[:, l, :], in0=st_h[:, l, :], in1=psb[pS][:, :], op=ALU.add), reads=["ps%d" % pS, "st_h%d" % l], writes=["st_h%d" % l])
                refresh_hz()
            for b, (c0, L) in enumerate(blocks):
                ssd_block(b, c0, L)
            siz, slz = load_wi_slot(l, 2048)
            for c in range(4):
                pi = proj_fm(siz, slz, c, T)
                A("act", lambda e, pi=pi: e.activation(out=zst[:, 0:T], in_=psb[pi][:, 0:T], func=AF.Silu), reads=["ps%d" % pi], writes=["zst"])
                A("dve", lambda e, c=c: e.tensor_tensor(out=ar[:, 6 + c, 0:T], in0=ar[:, 6 + c, 0:T], in1=zst[:, 0:T], op=ALU.mult), reads=["zst", ak(6 + c)], writes=[ak(6 + c)])
                A("act", lambda e, c=c: e.activation(out=pgb(c // 2)[:, (c % 2) * 512:(c % 2) * 512 + T], in_=ar[:, 6 + c, 0:T], func=AF.Square), reads=[ak(6 + c)], writes=[ak(c // 2)])
            for g in range(2):
                pi = nextps()
                for cc in range(2):
                    c = 2 * g + cc
                    A("pe", lambda e, c=c, cc=cc, pi=pi: e.matmul(psb[pi][:, 0:T], lhsT=onesb[:], rhs=pgb(c // 2)[:, (c % 2) * 512:(c % 2) * 512 + T], start=(cc == 0), stop=(cc == 1)),
                      reads=[ak(c // 2), "onesb"], writes=["ps%d" % pi])
                rstd_from_ps(pi, T, rs[:, g, 0:T], EPS * 256)
                for cc in range(2):
                    c = 2 * g + cc
                    A("dve", lambda e, c=c, g=g: e.scalar_tensor_tensor(out=mix_ap(8 + c, 0, T), in0=ar[:, 6 + c, 0:T], scalar=ngc[:, l, c:c + 1], in1=rs[:, g, 0:T], op0=ALU.mult, op1=ALU.mult),
                      reads=[ak(6 + c), "rs", "ngc"], writes=[mk_(8 + c)])
            bg_flush()
            ps_reserved.discard(7)
            p1, p2 = nextps(), nextps()
            for m in range(4):
                A("pe", lambda e, m=m: e.matmul(psb[p1][:, 0:T], lhsT=onesf[:], rhs=ar[:, 18 + m, 0:T], start=(m == 0), stop=(m == 3)),
                  reads=[ak(18 + m), "onesf"], writes=["ps%d" % p1])
            for m in range(4):
                tt = 16 + m % 2
                A("act", lambda e, m=m, tt=tt: e.activation(out=ar[:, tt, 0:T], in_=ar[:, 18 + m, 0:T], func=AF.Square), reads=[ak(18 + m)], writes=[ak(tt)])
                A("pe", lambda e, m=m, tt=tt: e.matmul(psb[p2][:, 0:T], lhsT=onesf[:], rhs=ar[:, tt, 0:T], start=(m == 0), stop=(m == 3)),
                  reads=[ak(tt), "onesf"], writes=["ps%d" % p2])
            mean = rs[:, 0, 0:T]; rstd = rs[:, 1, 0:T]
            A("dve", lambda e: e.tensor_scalar(out=mean, in0=psb[p1][:, 0:T], scalar1=1.0 / 512, scalar2=None, op0=ALU.mult), reads=["ps%d" % p1], writes=["rs"])
            A("dve", lambda e: e.tensor_tensor(out=rstd, in0=mean, in1=mean, op=ALU.mult), reads=["rs"], writes=["rs"])
            A("dve", lambda e: e.scalar_tensor_tensor(out=rstd, in0=psb[p2][:, 0:T], scalar=1.0 / 512, in1=rstd, op0=ALU.mult, op1=ALU.subtract),
              reads=["rs", "ps%d" % p2], writes=["rs"])
            A("dve", lambda e: e.tensor_scalar(out=rstd, in0=rstd, scalar1=EPS, scalar2=None, op0=ALU.add), reads=["rs"], writes=["rs"])
            A("act", lambda e: e.activation(out=rstd, in_=rstd, func=AF.Sqrt), reads=["rs"], writes=["rs"])
            A("dve", lambda e: e.reciprocal(out=rstd, in_=rstd), reads=["rs"], writes=["rs"])
            for m in range(4):
                A("dve", lambda e, m=m: e.tensor_tensor(out=ar[:, 18 + m, 0:T], in0=ar[:, 18 + m, 0:T], in1=mean, op=ALU.subtract), reads=["rs", ak(18 + m)], writes=[ak(18 + m)])
                A("dve", lambda e, m=m: e.tensor_tensor(out=ar[:, 18 + m, 0:T], in0=ar[:, 18 + m, 0:T], in1=rstd, op=ALU.mult), reads=["rs", ak(18 + m)], writes=[ak(18 + m)])
                A("act", lambda e, m=m: e.activation(out=mix_ap(4 + m, 0, T), in_=ar[:, 18 + m, 0:T], func=AF.Silu, bias=ppc(l, O_BLB + m), scale=ppc(l, O_BLG + m)),
                  reads=[ak(18 + m), "pp"], writes=[mk_(4 + m)])
            proj_postnorm(l, 3, T, wo_b[l], "wo%d" % l, NCH, lambda kc: mix_ap(kc, 0, T), lambda kc: [mk_(kc)], 0, 8, s_order=[0, 2, 3, 1])

        ngc = sb("ngc", [128, 2, 4])
        for l in range(2):
            A("dve", lambda e, l=l: e.tensor_scalar(out=ngc[:, l, :], in0=pp[:, l * NPP + O_CNG: l * NPP + O_CNG + 4], scalar1=16.0, scalar2=None, op0=ALU.mult),
              reads=["pp"], writes=["ngc"])

        def run_tile(T, blocks, sample, x_src, y_dst):
            xv = x_src.rearrange("(c p) t -> p c t", p=128)
            for q in range(4):
                A("sp", lambda e, q=q: e.dma_start(out=xT[:, 4 * q:4 * q + 4, 0:T], in_=xv[:, 4 * q:4 * q + 4, :]), writes=["x%d" % c for c in range(4 * q, 4 * q + 4)], dma="xin%d" % q)
            for l in range(2):
                ffn(l, 0, T)
                mixers(l, T, blocks, sample)
                ffn(l, 1, T, stage=(l == 1))
            yv = y_dst.rearrange("(c p) t -> p c t", p=128)
            for q in range(4):
                p0 = STAGE_PAGES[4 * q]
                A("pool", lambda e, q=q, p0=p0: e.dma_start(out=yv[:, 4 * q:4 * q + 4, :], in_=ar[:, p0:p0 + 4, 0:T]), reads=aks(p0, p0 + 4), dma="yout%d" % q)

        def store_states(tag):
            for l in range(2):
                A("pool", lambda e, l=l: e.dma_start(out=o_cb[tag][l].rearrange("(c p) k -> p c k", p=128), in_=st_b[:, l, :, :]), reads=["st_b%d" % l], dma="os0")
                A("pool", lambda e, l=l: e.dma_start(out=o_cc[tag][l].rearrange("(c p) k -> p c k", p=128), in_=st_c[:, l, :, :]), reads=["st_c%d" % l], dma="os1")
                A("pool", lambda e, l=l: e.dma_start(out=o_cd[tag][l].rearrange("(c p) k -> p c k", p=128), in_=st_d[:, l, :, :]), reads=["st_d%d" % l], dma="os2")
                A("pool", lambda e, l=l: e.dma_start(out=o_h[tag][l], in_=st_h[:, l, :]), reads=["st_h%d" % l], dma="os3")

        if NP > 0:
            for l in range(2):
                A("dve", lambda e, l=l: e.memset(st_b[:, l, :, :], 0.0), writes=["st_b%d" % l])
                A("dve", lambda e, l=l: e.memset(st_c[:, l, :, :], 0.0), writes=["st_c%d" % l])
                A("dve", lambda e, l=l: e.memset(st_d[:, l, :, :], 0.0), writes=["st_d%d" % l])
                A("dve", lambda e, l=l: e.memset(st_h[:, l, :], 0.0), writes=["st_h%d" % l])
            blocks = [(i * 128, 128) for i in range(4)]
            for i in range(NP):
                run_tile(TP, blocks, False, xT_d[:, i * TP:(i + 1) * TP], yT_d[:, i * TP:(i + 1) * TP])
            store_states("p")
        for l in range(2):
            A("sp", lambda e, l=l: e.dma_start(out=st_b[:, l, :, :], in_=cb_in[l].rearrange("(c p) k -> p c k", p=128)), writes=["st_b%d" % l], dma="is0")
            A("sp", lambda e, l=l: e.dma_start(out=st_c[:, l, :, :], in_=cc_in[l].rearrange("(c p) k -> p c k", p=128)), writes=["st_c%d" % l], dma="is1")
            A("sp", lambda e, l=l: e.dma_start(out=st_d[:, l, :, :], in_=cd_in[l].rearrange("(c p) k -> p c k", p=128)), writes=["st_d%d" % l], dma="is2")
            A("sp", lambda e, l=l: e.dma_start(out=st_h[:, l, :], in_=h_in[l]), writes=["st_h%d" % l], dma="is3")
        run_tile(16, [(0, 16)], True, xsT_d, ysT_d)
        store_states("s")
        S.emit(nc, es)
    return nc


def _prep_shared(inp):
    f = np.float32
    wfi = np.ascontiguousarray(inp["ffn_w_in"].reshape(2, 2, D, 2, NHC, 128).transpose(0, 1, 2, 4, 3, 5).reshape(2, 2, D, 2 * DFF), dtype=f)
    pp = np.zeros((128, 2, NPP), f)
    for l in range(2):
        pp[:, l, O_NG:O_NG + 96] = inp["norm_g"][l].reshape(6, 16, 128).transpose(2, 0, 1).reshape(128, 96)
        pp[:, l, O_BCW:O_BCW + 124] = inp["b_conv_w"][l].reshape(31, 4, 128).transpose(2, 1, 0).reshape(128, 124)
        pp[:, l, O_BCB:O_BCB + 4] = inp["b_conv_b"][l].reshape(4, 128).T
        pp[:, l, O_BLG:O_BLG + 4] = inp["b_ln_g"][l].reshape(4, 128).T
        pp[:, l, O_BLB:O_BLB + 4] = inp["b_ln_b"][l].reshape(4, 128).T
        pp[:, l, O_CCW:O_CCW + 32] = inp["c_conv_w"][l].reshape(4, 8, 128).transpose(2, 1, 0).reshape(128, 32)
        pp[:, l, O_CCB:O_CCB + 8] = inp["c_conv_b"][l].reshape(8, 128).T
        pp[:, l, O_CDSK:O_CDSK + 4] = np.repeat(inp["c_d"][l], 64).reshape(4, 128).T
        pp[:, l, O_CNG:O_CNG + 4] = inp["c_norm_g"][l].reshape(4, 128).T
        pp[:, l, O_DCW:O_DCW + 12] = inp["d_conv_w"][l].reshape(3, 4, 128).transpose(2, 1, 0).reshape(128, 12)
    bcr = np.zeros((1, 2, BCW), f)
    for l in range(2):
        bcr[0, l, 0:512] = inp["a_ln_g"][l]; bcr[0, l, 512:1024] = inp["a_ln_b"][l]
        bcr[0, l, 1024:1032] = inp["c_dt_bias"][l]; bcr[0, l, 1032:1040] = inp["c_a_log"][l]
    wsT = np.ascontiguousarray(inp["a_ws"].transpose(3, 0, 1, 2).reshape(128, 1024), dtype=f)
    bsr = np.ascontiguousarray(inp["a_bs"].reshape(1, 1024), dtype=f)
    return {"wfi": wfi, "wfo": np.ascontiguousarray(inp["ffn_w_out"], dtype=f), "wi": np.ascontiguousarray(inp["w_in"], dtype=f),
            "wo": np.ascontiguousarray(inp["w_out"], dtype=f), "pp": pp.reshape(128, 2 * NPP), "bcr": bcr.reshape(1, 2 * BCW),
            "wsT": wsT, "bsr": bsr}


def run(inp, NP, n_cores, n_prompt):
    f = np.float32
    shared = _prep_shared(inp)
    nc = build(NP)
    in_maps = []
    NTOK = max(NP, 1) * TP
    pcores = [0, 1, 4, 5][:n_prompt] if n_cores == 8 else list(range(n_prompt))
    for c in range(n_cores):
        m = dict(shared)
        if c in pcores and NP > 0:
            m["xT"] = np.ascontiguousarray(inp["x_prompt"][pcores.index(c)].T, dtype=f)
        else:
            m["xT"] = np.zeros((D, NTOK), f)
        m["xsT"] = np.ascontiguousarray(inp["x_sample"][c].T, dtype=f)
        m["cb_in"] = np.ascontiguousarray(inp["cache_conv_b"][:, c].transpose(0, 2, 1), dtype=f)
        m["cc_in"] = np.ascontiguousarray(inp["cache_conv_c"][:, c].transpose(0, 2, 1), dtype=f)
        m["cd_in"] = np.ascontiguousarray(inp["cache_conv_d"][:, c].transpose(0, 2, 1), dtype=f)
        m["h_in"] = np.ascontiguousarray(inp["state_ssm"][:, c].reshape(2, 512, 128).transpose(0, 2, 1), dtype=f)
        in_maps.append(m)
    res = run_bass_kernel_spmd(nc, in_maps, core_ids=list(range(n_cores)))
    R = res.results

    def st(tag, cores):
        cb = np.stack([R[c][tag + "_cb"].transpose(0, 2, 1) for c in cores], 1)
        cc = np.stack([R[c][tag + "_cc"].transpose(0, 2, 1) for c in cores], 1)
        hh = np.stack([R[c][tag + "_h"].transpose(0, 2, 1).reshape(2, 8, 64, 128) for c in cores], 1)
        cd = np.stack([R[c][tag + "_cd"].transpose(0, 2, 1) for c in cores], 1)
        return [np.ascontiguousarray(a, dtype=f) for a in (cb, cc, hh, cd)]

    pc = pcores; sc = list(range(n_cores))
    y_prompt = np.ascontiguousarray(np.stack([R[c]["yT"].T for c in pc], 0), dtype=f) if NP > 0 else None
    y_sample = np.ascontiguousarray(np.stack([R[c]["ysT"].T for c in sc], 0), dtype=f)
    p = st("p", pc) if NP > 0 else [None] * 4
    s = st("s", sc)
    s_v = np.ascontiguousarray(np.stack([R[c]["s_v"] for c in sc], 1), dtype=f)
    return (y_prompt, y_sample, p[0], p[1], p[2], p[3], s_v, s[0], s[1], s[2], s[3])


def kernel(**inputs):
    return run(inputs, 16, 8, 4)
```

```python
import contextlib
import numpy as np
import concourse.bass as bass
import concourse.mybir as mybir
from concourse.bass_utils import run_bass_kernel_spmd

F32 = mybir.dt.float32
BF16 = mybir.dt.bfloat16
AF = mybir.ActivationFunctionType
ALU = mybir.AluOpType

D = 2048
DFF = 5632
NCH = 16
NHC = 44
INW = 5128
TP = 512
EPS = 1e-6
NPP = 292
O_NG, O_BCW, O_BCB, O_BLG, O_BLB, O_CCW, O_CCB, O_CDSK, O_CNG, O_DCW = 0, 96, 220, 224, 228, 232, 264, 272, 276, 280
FAST_RSTD = False
BCW = 1040


class _Op:
    __slots__ = ("eng", "fn", "deps", "signal", "dma", "lane", "semval", "eidx", "order")


class Sched:
    ENGS = ("pe", "act", "dve", "pool", "sp")

    def __init__(self):
        self.ops = []
        self.per = {e: [] for e in self.ENGS}
        self.last_w = {}
        self.readers = {}
        self.lane_last = {}

    def add(self, eng, fn, reads=(), writes=(), dma=None):
        op = _Op()
        op.eng = eng; op.fn = fn; op.signal = False; op.dma = dma is not None
        op.lane = dma; op.semval = None; op.order = len(self.ops)
        deps = []
        lw = self.last_w; rd = self.readers
        for k in reads:
            w = lw.get(k)
            if w is not None:
                deps.append(w)
        for k in writes:
            w = lw.get(k)
            if w is not None:
                deps.append(w)
            r = rd.get(k)
            if r:
                deps.extend(r)
        if op.dma:
            p = self.lane_last.get(dma)
            if p is not None:
                deps.append(p)
            self.lane_last[dma] = op
        for k in reads:
            r = rd.get(k)
            if r is None:
                rd[k] = [op]
            else:
                r.append(op)
        for k in writes:
            lw[k] = op
            rd[k] = []
        op.eidx = len(self.per[eng])
        best = {}
        for d in deps:
            if d is op:
                continue
            if d.dma:
                key = ("L", d.lane)
                if key not in best or best[key].order < d.order:
                    best[key] = d
            else:
                if d.eng == eng:
                    if eng == "pe" or eng == "sp":
                        continue
                    if op.eidx - d.eidx > 6:
                        continue
                key = ("E", d.eng)
                if key not in best or best[key].eidx < d.eidx:
                    best[key] = d
        op.deps = list(best.values())
        self.per[eng].append(op)
        self.ops.append(op)
        return op

    def emit(self, nc, es):
        for op in self.ops:
            for d in op.deps:
                d.signal = True
        esem = {e: es.enter_context(nc.semaphore("s_" + e)) for e in self.ENGS}
        lanes = {}
        lane_cnt = {}
        for op in self.ops:
            if op.dma:
                if op.lane not in lanes:
                    lanes[op.lane] = es.enter_context(nc.semaphore("l_%d" % len(lanes)))
                    lane_cnt[op.lane] = 0
                lane_cnt[op.lane] += 16
                op.semval = lane_cnt[op.lane]
        for e in self.ENGS:
            c = 0
            for op in self.per[e]:
                if not op.dma and op.signal:
                    c += 1
                    op.semval = c
        engobj = {"pe": "tensor", "act": "scalar", "dve": "vector", "pool": "gpsimd", "sp": "sync"}
        block = es.enter_context(nc.Block())

        def mk(e):
            ops = self.per[e]

            def body(eng):
                seen = {}
                for op in ops:
                    for d in op.deps:
                        if d.dma:
                            s = lanes[d.lane]; key = ("L", d.lane)
                        else:
                            s = esem[d.eng]; key = ("E", d.eng)
                        if seen.get(key, 0) >= d.semval:
                            continue
                        seen[key] = d.semval
                        eng.wait_ge(s, d.semval)
                    ins = op.fn(eng)
                    if op.dma:
                        ins.then_inc(lanes[op.lane], 16)
                    elif op.signal:
                        ins.then_inc(esem[e], 1)
                if e == "sp":
                    for ln, cnt in lane_cnt.items():
                        eng.wait_ge(lanes[ln], cnt)
            return body

        for e in self.ENGS:
            if self.per[e]:
                getattr(block, engobj[e])(mk(e))


def build(NP):
    nc = bass.Bass("TRN2", target_bir_lowering=False)
    S = Sched()
    A = S.add

    def din(name, shape):
        return nc.dram_tensor(name, shape, F32, kind="ExternalInput").ap()

    def dout(name, shape):
        return nc.dram_tensor(name, shape, F32, kind="ExternalOutput").ap()

    NTOK = max(NP, 1) * TP
    xT_d = din("xT", [D, NTOK])
    xsT_d = din("xsT", [D, 16])
    cb_in = din("cb_in", [2, 512, 30]); cc_in = din("cc_in", [2, 1024, 3])
    cd_in = din("cd_in", [2, 512, 2]); h_in = din("h_in", [2, 128, 512])
    wfi_d = din("wfi", [2, 2, D, 2 * DFF]); wfo_d = din("wfo", [2, 2, DFF, D])
    wi_d = din("wi", [2, D, INW]); wo_d = din("wo", [2, D, D])
    pp_d = din("pp", [128, 2 * NPP]); bcr_d = din("bcr", [1, 2 * BCW])
    wsT_d = din("wsT", [128, 2 * 4 * 128]); bsr_d = din("bsr", [1, 1024])

    yT_d = dout("yT", [D, NTOK]); ysT_d = dout("ysT", [D, 16])
    o_cb = {"p": dout("p_cb", [2, 512, 30]), "s": dout("s_cb", [2, 512, 30])}
    o_cc = {"p": dout("p_cc", [2, 1024, 3]), "s": dout("s_cc", [2, 1024, 3])}
    o_cd = {"p": dout("p_cd", [2, 512, 2]), "s": dout("s_cd", [2, 512, 2])}
    o_h = {"p": dout("p_h", [2, 128, 512]), "s": dout("s_h", [2, 128, 512])}
    s_v = dout("s_v", [2, 16, 512])

    def dscr(name, shape):
        return nc.dram_tensor(name, shape, BF16, kind="Internal").ap()

    wfi_b = [[dscr("wfib%d%d" % (l, f), [D, 2 * DFF]) for f in range(2)] for l in range(2)]
    wfo_b = [[dscr("wfob%d%d" % (l, f), [DFF, D]) for f in range(2)] for l in range(2)]
    wi_b = [dscr("wib%d" % l, [D, INW]) for l in range(2)]
    wo_b = [dscr("wob%d" % l, [D, D]) for l in range(2)]

    es = contextlib.ExitStack()
    with es:
        def sb(name, shape, dt=F32):
            return es.enter_context(nc.sbuf_tensor(name, shape, dt))

        xT = sb("xT_s", [128, NCH, TP])
        hT = sb("hT_s", [128, NCH, TP], BF16)
        ar = sb("arena", [128, 30, 512])
        slots = [sb("slot%d" % i, [128, 8192], BF16) for i in range(3)]
        cbuf = sb("cbuf", [128, 4, 516])
        cbufB = sb("cbufB", [128, 4, 544])
        hzt = sb("hzt", [128, 1024], BF16)
        zst = sb("zst", [128, 512])
        rs = sb("rs", [128, 2, 512])
        identf = sb("identf", [128, 128]); identb = sb("identb", [128, 128], BF16)
        onesf = sb("onesf", [128, 128]); onesb = sb("onesb", [128, 128], BF16)
        tri = sb("tri", [128, 128]); maskneg = sb("maskneg", [128, 128])
        wsTb = sb("wsTb", [128, 2, 4, 128], BF16)
        bsh = sb("bsh", [1, 1024], BF16); bsl = sb("bsl", [1, 1024], BF16)
        bc = sb("bc", [128, 2 * BCW])
        pp = sb("pp_s", [128, 2 * NPP]); ngs = sb("ngs", [128, 2, 96])
        wdt = sb("wdt", [128, 2, 16, 8], BF16)
        st_b = sb("st_b", [128, 2, 4, 30]); st_c = sb("st_c", [128, 2, 8, 3])
        st_d = sb("st_d", [128, 2, 4, 2]); st_h = sb("st_h", [128, 2, 512])
        sm = sb("sm", [128, 192])
        bnst = sb("bnst", [128, 8])
        psb = [es.enter_context(nc.psum_tensor("ps%d" % i, [128, 512], F32)) for i in range(8)]

        def pg(i):
            return ar[:, i, :]

        def pgb(i):
            return ar[:, i, :].bitcast(BF16)

        def ak(i):
            return "a%d" % i

        def aks(a, b):
            return [ak(i) for i in range(a, b)]

        def ppc(l, off):
            return pp[:, l * NPP + off: l * NPP + off + 1]

        pcount = [0]

        ps_reserved = set()

        def nextps():
            while True:
                i = pcount[0] % 8
                pcount[0] += 1
                if i not in ps_reserved:
                    return i

        bg = {"gen": None}

        def bg_step(n):
            g = bg["gen"]
            if g is None:
                return
            for _ in range(n):
                try:
                    next(g)
                except StopIteration:
                    bg["gen"] = None
                    return

        def bg_flush():
            while bg["gen"] is not None:
                bg_step(8)

        scount = [0]

        def nextslot():
            i = scount[0] % 3
            scount[0] += 1
            return i

        A("dve", lambda e: e.memset(onesf[:], 1.0), writes=["onesf"])
        A("dve", lambda e: e.memset(onesb[:], 1.0), writes=["onesb"])
        A("pool", lambda e: e.memset(identf[:], 1.0), writes=["identf"])
        A("pool", lambda e: e.affine_select(out=identf[:], in_=identf[:], pattern=[[-1, 128]], compare_op=ALU.is_equal,
                                            fill=0.0, base=0, channel_multiplier=1), reads=["identf"], writes=["identf"])
        A("dve", lambda e: e.tensor_copy(out=identb[:], in_=identf[:]), reads=["identf"], writes=["identb"])
        A("pool", lambda e: e.memset(tri[:], 1.0), writes=["tri"])
        A("pool", lambda e: e.affine_select(out=tri[:], in_=tri[:], pattern=[[1, 128]], compare_op=ALU.is_ge,
                                            fill=0.0, base=0, channel_multiplier=-1), reads=["tri"], writes=["tri"])
        A("pool", lambda e: e.memset(maskneg[:], 0.0), writes=["maskneg"])
        A("pool", lambda e: e.affine_select(out=maskneg[:], in_=maskneg[:], pattern=[[1, 128]], compare_op=ALU.is_ge,
                                            fill=-30000.0, base=0, channel_multiplier=-1), reads=["maskneg"], writes=["maskneg"])
        A("dve", lambda e: e.memset(hzt[:, :], 0.0), writes=["hz"])
        A("sp", lambda e: e.dma_start(out=pp[:], in_=pp_d[:, :]), writes=["pp"], dma="c0")
        A("sp", lambda e: e.dma_start(out=bc[:], in_=bcr_d.partition_broadcast(128)), writes=["bc"], dma="c1")
        wst_f = ar[:, 0:2, :].rearrange("p a (h t) -> p a h t", h=4)
        A("sp", lambda e: e.dma_start(out=ar[:, 0:2, :], in_=wsT_d.rearrange("p (a f) -> p a f", a=2)), writes=aks(0, 2), dma="c2")
        A("pool", lambda e: e.affine_select(out=wst_f, in_=wst_f, pattern=[[0, 2], [0, 4], [1, 128]], compare_op=ALU.is_ge,
                                            fill=0.0, base=0, channel_multiplier=-1), reads=aks(0, 2), writes=aks(0, 2))
        A("dve", lambda e: e.tensor_copy(out=wsTb[:], in_=wst_f), reads=aks(0, 2), writes=["wsTb"])
        A("sp", lambda e: e.dma_start(out=ar[0:1, 2, :], in_=bsr_d[:, 0:512]), writes=[ak(2)], dma="c3")
        A("sp", lambda e: e.dma_start(out=ar[0:1, 3, :], in_=bsr_d[:, 512:1024]), writes=[ak(3)], dma="c3")
        for a in range(2):
            A("dve", lambda e, a=a: e.tensor_copy(out=bsh[0:1, a * 512:(a + 1) * 512], in_=ar[0:1, 2 + a, :]), reads=[ak(2 + a)], writes=["bsh"])
            A("dve", lambda e, a=a: e.tensor_copy(out=ar[0:1, 4 + a, :], in_=bsh[0:1, a * 512:(a + 1) * 512]), reads=["bsh"], writes=[ak(4 + a)])
            A("dve", lambda e, a=a: e.tensor_tensor(out=ar[0:1, 4 + a, :], in0=ar[0:1, 2 + a, :], in1=ar[0:1, 4 + a, :], op=ALU.subtract),
              reads=[ak(2 + a), ak(4 + a)], writes=[ak(4 + a)])
            A("dve", lambda e, a=a: e.tensor_copy(out=bsl[0:1, a * 512:(a + 1) * 512], in_=ar[0:1, 4 + a, :]), reads=[ak(4 + a)], writes=["bsl"])
        for l in range(2):
            o = l * BCW + 1032
            A("act", lambda e, o=o: e.activation(out=bc[:, o:o + 8], in_=bc[:, o:o + 8], func=AF.Exp), reads=["bc"], writes=["bc"])
            A("dve", lambda e, o=o: e.tensor_scalar(out=bc[:, o:o + 8], in0=bc[:, o:o + 8], scalar1=-1.0, scalar2=None, op0=ALU.mult),
              reads=["bc"], writes=["bc"])
        sD = float(np.sqrt(D))
        for l in range(2):
            for i in range(6):
                c = sD * (0.5 if i in (1, 5) else 1.0)
                A("dve", lambda e, l=l, i=i, c=c: e.tensor_scalar(out=ngs[:, l, i * 16:(i + 1) * 16], in0=pp[:, l * NPP + i * 16: l * NPP + (i + 1) * 16],
                                                                  scalar1=c, scalar2=None, op0=ALU.mult), reads=["pp"], writes=["ngs"])
        wdt_loaded = set()

        castlane = [0]

        def cast_rows(name, dst, src, rows, rstep):
            for i, r0 in enumerate(range(0, rows, rstep)):
                r1 = min(rows, r0 + rstep)
                k = "w:%s:%d" % (name, i)
                ln = "cast%d" % (castlane[0] % 4)
                castlane[0] += 1
                A("pool", lambda e, r0=r0, r1=r1: e.dma_start(out=dst[r0:r1, :], in_=src[r0:r1, :]), writes=[k], dma=ln)

        def cast_cols(name, dst, src, bounds):
            for (c0, c1) in bounds:
                k = "w:%s:%d" % (name, c0)
                ln = "cast%d" % (castlane[0] % 4)
                castlane[0] += 1
                A("pool", lambda e, c0=c0, c1=c1: e.dma_start(out=dst[:, c0:c1], in_=src[:, c0:c1]), writes=[k], dma=ln)

        WI_BOUNDS = [(0, 512), (512, 1024), (1024, 1536), (1536, 2048), (2048, 2560), (2560, 3072), (3072, 3584), (3584, 3592),
                     (3592, 4104), (4104, 4616), (4616, 5128)]
        FI_BOUNDS = [(g * 512, (g + 1) * 512) for g in range(22)]
        for l in range(2):
            cast_cols("fi%d0" % l, wfi_b[l][0], wfi_d[l, 0], FI_BOUNDS)
            cast_rows("fo%d0" % l, wfo_b[l][0], wfo_d[l, 0], DFF, 512)
            cast_cols("wi%d" % l, wi_b[l], wi_d[l], WI_BOUNDS)
            cast_rows("wo%d" % l, wo_b[l], wo_d[l], D, 512)
            cast_cols("fi%d1" % l, wfi_b[l][1], wfi_d[l, 1], FI_BOUNDS)
            cast_rows("fo%d1" % l, wfo_b[l][1], wfo_d[l, 1], DFF, 512)

        def rstd_from_ps(pi, T, dst, addc):
            if FAST_RSTD:
                A("act", lambda e: e.activation(out=dst, in_=psb[pi][:, 0:T], func=AF.Abs_reciprocal_sqrt, bias=addc), reads=["ps%d" % pi], writes=["rs"])
                return
            A("act", lambda e: e.activation(out=dst, in_=psb[pi][:, 0:T], func=AF.Sqrt, bias=float(addc)), reads=["ps%d" % pi], writes=["rs"])
            A("dve", lambda e: e.reciprocal(out=dst, in_=dst), reads=["rs"], writes=["rs"])

        def rmsnorm_to_h(l, gi, T):
            for q in range(4):
                A("act", lambda e, q=q: e.activation(out=hT[:, 4 * q:4 * q + 4, 0:T], in_=xT[:, 4 * q:4 * q + 4, 0:T], func=AF.Square),
                  reads=["x%d" % c for c in range(4 * q, 4 * q + 4)], writes=["h%d" % c for c in range(4 * q, 4 * q + 4)])
            pi = nextps()
            for c in range(NCH):
                A("pe", lambda e, c=c: e.matmul(psb[pi][:, 0:T], lhsT=onesb[:], rhs=hT[:, c, 0:T], start=(c == 0), stop=(c == NCH - 1)),
                  reads=["h%d" % c, "onesb"], writes=["ps%d" % pi])
            rstd_from_ps(pi, T, rs[:, 0, 0:T], EPS * D)
            for c in range(NCH):
                A("dve", lambda e, c=c: e.scalar_tensor_tensor(out=hT[:, c, 0:T], in0=xT[:, c, 0:T], scalar=ngs[:, l, gi * 16 + c: gi * 16 + c + 1],
                                                               in1=rs[:, 0, 0:T], op0=ALU.mult, op1=ALU.mult),
                  reads=["x%d" % c, "rs", "ngs"], writes=["h%d" % c])

        def proj_postnorm(l, gi, T, wsrc, wk, nK, rhs_ap, rhs_keys, yh0, tp0, s_order=None):
            wv = wsrc.rearrange("(k p) n -> p k n", p=128)
            if s_order is None:
                s_order = list(range(nK // 4))
            kfirst = 4 * s_order[0]; klast = 4 * s_order[-1] + 3
            tmp0 = ar[:, tp0, 0:T]; tk0 = ak(tp0)
            for h in range(2):
                for s in s_order:
                    si = nextslot()
                    sl = slots[si][:, :].rearrange("p (j n) -> p j n", j=4)
                    A("sp", lambda e, sl=sl, s=s, h=h: e.dma_start(out=sl[:, :, 0:1024], in_=wv[:, 4 * s:4 * s + 4, h * 1024:(h + 1) * 1024]),
                      reads=["w:%s:%d" % (wk, s)], writes=["slot%d" % si], dma="slot%d" % si)
                    for jj in range(4):
                        kc = 4 * s + jj
                        for m in range(8):
                            A("pe", lambda e, sl=sl, jj=jj, m=m, kc=kc: e.matmul(psb[m][:, 0:T], lhsT=sl[:, jj, m * 128:(m + 1) * 128], rhs=rhs_ap(kc),
                                                                              start=(kc == kfirst), stop=(kc == klast)),
                              reads=["slot%d" % si] + rhs_keys(kc), writes=["ps%d" % m])
                if h == 0:
                    for m in range(8):
                        if m % 2 == 0:
                            A("act", lambda e, m=m: e.activation(out=ar[:, yh0 + m, 0:T], in_=psb[m][:, 0:T], func=AF.Copy),
                              reads=["ps%d" % m], writes=[ak(yh0 + m)])
                        else:
                            A("dve", lambda e, m=m: e.tensor_copy(out=ar[:, yh0 + m, 0:T], in_=psb[m][:, 0:T]),
                              reads=["ps%d" % m], writes=[ak(yh0 + m)])
                    for m in range(8):
                        A("act", lambda e, m=m: e.activation(out=hT[:, m, 0:T], in_=ar[:, yh0 + m, 0:T], func=AF.Square),
                          reads=[ak(yh0 + m)], writes=["h%d" % m])
            A("act", lambda e: e.activation(out=tmp0, in_=psb[7][:, 0:T], func=AF.Copy), reads=["ps7"], writes=[tk0])
            for m in range(7):
                A("act", lambda e, m=m: e.activation(out=hT[:, 8 + m, 0:T], in_=psb[m][:, 0:T], func=AF.Square),
                  reads=["ps%d" % m], writes=["h%d" % (8 + m)])
            A("act", lambda e: e.activation(out=hT[:, 15, 0:T], in_=tmp0, func=AF.Square), reads=[tk0], writes=["h15"])
            for c in range(NCH):
                A("pe", lambda e, c=c: e.matmul(psb[7][:, 0:T], lhsT=onesb[:], rhs=hT[:, c, 0:T], start=(c == 0), stop=(c == NCH - 1)),
                  reads=["h%d" % c, "onesb"], writes=["ps7"])
            rstd_from_ps(7, T, rs[:, 0, 0:T], EPS * D)
            for c in range(NCH):
                if c < 8:
                    src = ar[:, yh0 + c, 0:T]; sk = ak(yh0 + c)
                elif c < 15:
                    src = psb[c - 8][:, 0:T]; sk = "ps%d" % (c - 8)
                else:
                    src = tmp0; sk = tk0
                tt = tp0 + 1 + (c % 3)
                A("dve", lambda e, c=c, src=src, tt=tt: e.scalar_tensor_tensor(out=ar[:, tt, 0:T], in0=src, scalar=ngs[:, l, gi * 16 + c: gi * 16 + c + 1],
                                                                               in1=rs[:, 0, 0:T], op0=ALU.mult, op1=ALU.mult),
                  reads=[sk, "rs", "ngs"], writes=[ak(tt)])
                A("dve", lambda e, c=c, tt=tt: e.tensor_tensor(out=xT[:, c, 0:T], in0=xT[:, c, 0:T], in1=ar[:, tt, 0:T], op=ALU.add),
                  reads=[ak(tt), "x%d" % c], writes=["x%d" % c])

        def hid_ap(j, T):
            return pgb(j // 2)[:, (j % 2) * 512:(j % 2) * 512 + T]

        def ffn(l, f, T):
            rmsnorm_to_h(l, 0 if f == 0 else 4, T)
            wv = wfi_b[l][f].rearrange("(k p) n -> p k n", p=128)
            wk = "fi%d%d" % (l, f)
            hkeys = ["h%d" % c for c in range(NCH)]
            def ffn_evac(j, pg_, pu_):
                tt = 22 + (j % 3)
                A("act", lambda e: e.activation(out=ar[:, tt, 0:T], in_=psb[pg_][:, 0:T], func=AF.Silu),
                  reads=["ps%d" % pg_], writes=[ak(tt)])
                A("dve", lambda e: e.tensor_tensor(out=hid_ap(j, T), in0=ar[:, tt, 0:T], in1=psb[pu_][:, 0:T], op=ALU.mult),
                  reads=["ps%d" % pu_, ak(tt)], writes=[ak(j // 2)])

            combos = []
            for g in range(2):
                si = nextslot()
                sl = slots[si][:, :].rearrange("p (k n) -> p k n", k=16)
                A("sp", lambda e, sl=sl, g=g: e.dma_start(out=sl, in_=wv[:, :, g * 512:(g + 1) * 512]), reads=["w:%s:%d" % (wk, g * 512)], writes=["slot%d" % si], dma="slot%d" % si)
                for jq in range(2):
                    for off in (jq * 256, jq * 256 + 128):
                        combos.append((si, sl, off, nextps()))
            for kc in range(NCH):
                for (si, sl, off, pi) in combos:
                    A("pe", lambda e, sl=sl, pi=pi, off=off, kc=kc: e.matmul(psb[pi][:, 0:T], lhsT=sl[:, kc, off:off + 128], rhs=hT[:, kc, 0:T],
                                                                          start=(kc == 0), stop=(kc == NCH - 1)),
                      reads=["slot%d" % si, "h%d" % kc], writes=["ps%d" % pi])
            for j in range(4):
                ffn_evac(j, combos[2 * j][3], combos[2 * j + 1][3])
            for g in range(2, NHC // 2):
                si = nextslot()
                sl = slots[si][:, :].rearrange("p (k n) -> p k n", k=16)
                A("sp", lambda e, sl=sl, g=g: e.dma_start(out=sl, in_=wv[:, :, g * 512:(g + 1) * 512]), reads=["w:%s:%d" % (wk, g * 512)], writes=["slot%d" % si], dma="slot%d" % si)
                for jq in range(2):
                    j = 2 * g + jq
                    pg_, pu_ = nextps(), nextps()
                    for (pi, off) in ((pg_, jq * 256), (pu_, jq * 256 + 128)):
                        for kc in range(NCH):
                            A("pe", lambda e, sl=sl, pi=pi, off=off, kc=kc: e.matmul(psb[pi][:, 0:T], lhsT=sl[:, kc, off:off + 128], rhs=hT[:, kc, 0:T],
                                                                                  start=(kc == 0), stop=(kc == NCH - 1)),
                              reads=["slot%d" % si, "h%d" % kc], writes=["ps%d" % pi])
                    tt = 22 + (j % 3)
                    A("act", lambda e, pg_=pg_, tt=tt: e.activation(out=ar[:, tt, 0:T], in_=psb[pg_][:, 0:T], func=AF.Silu),
                      reads=["ps%d" % pg_], writes=[ak(tt)])
                    A("dve", lambda e, pu_=pu_, tt=tt, j=j: e.tensor_tensor(out=hid_ap(j, T), in0=ar[:, tt, 0:T], in1=psb[pu_][:, 0:T], op=ALU.mult),
                      reads=["ps%d" % pu_, ak(tt)], writes=[ak(j // 2)])
            proj_postnorm(l, 1 if f == 0 else 5, T, wfo_b[l][f], "fo%d%d" % (l, f), NHC,
                          lambda kc: hid_ap(kc, T), lambda kc: [ak(kc // 2)], 22, 8)

        def mix_ap(c, T0, T1):
            return pgb(22 + c // 2)[:, (c % 2) * 512 + T0:(c % 2) * 512 + T1]

        def mk_(c):
            return ak(22 + c // 2)

        def load_wi_slot(l, col0, ncols=512):
            si = nextslot()
            sl = slots[si][:, :].rearrange("p (k n) -> p k n", k=16)
            wv = wi_b[l].rearrange("(k p) n -> p k n", p=128)
            A("sp", lambda e: e.dma_start(out=sl[:, :, 0:ncols], in_=wv[:, :, col0:col0 + ncols]), reads=["w:wi%d:%d" % (l, col0)], writes=["slot%d" % si], dma="slot%d" % si)
            return si, sl

        def proj_fm(si, sl, m, T):
            pi = nextps()
            for kc in range(NCH):
                A("pe", lambda e, kc=kc: e.matmul(psb[pi][:, 0:T], lhsT=sl[:, kc, m * 128:(m + 1) * 128], rhs=hT[:, kc, 0:T],
                                                  start=(kc == 0), stop=(kc == NCH - 1)),
                  reads=["slot%d" % si, "h%d" % kc], writes=["ps%d" % pi])
            bg_step(4)
            return pi

        def conv_taps(eng, cb, cbk, l, K, woff, nchk, T, acc_page0, boff=None, chunk0=0):
            for k in range(K):
                for m in range(nchk):
                    cm = chunk0 + m
                    accm = ar[:, acc_page0 + m, 0:T]
                    wk_ = ppc(l, woff + cm * K + k)
                    if k == 0 and boff is not None:
                        A(eng, lambda e, m=m, accm=accm, wk_=wk_, cm=cm: e.tensor_scalar(out=accm, in0=cb[:, m, 0:T], scalar1=wk_, scalar2=ppc(l, boff + cm),
                                                                                       op0=ALU.mult, op1=ALU.add),
                          reads=[cbk % m, "pp"], writes=[ak(acc_page0 + m)])
                    elif k == 0:
                        A(eng, lambda e, m=m, accm=accm, wk_=wk_: e.tensor_scalar(out=accm, in0=cb[:, m, 0:T], scalar1=wk_, scalar2=None, op0=ALU.mult),
                          reads=[cbk % m, "pp"], writes=[ak(acc_page0 + m)])
                    elif eng == "pool":
                        tp_ = 16 + (m % 2)
                        A(eng, lambda e, m=m, wk_=wk_, k=k, tp_=tp_: e.tensor_scalar(out=ar[:, tp_, 0:T], in0=cb[:, m, k:k + T], scalar1=wk_, scalar2=None, op0=ALU.mult),
                          reads=[cbk % m, "pp"], writes=[ak(tp_)])
                        A(eng, lambda e, accm=accm, tp_=tp_: e.tensor_tensor(out=accm, in0=accm, in1=ar[:, tp_, 0:T], op=ALU.add),
                          reads=[ak(tp_), ak(acc_page0 + m)], writes=[ak(acc_page0 + m)])
                    else:
                        A(eng, lambda e, m=m, accm=accm, wk_=wk_, k=k: e.scalar_tensor_tensor(out=accm, in0=cb[:, m, k:k + T], scalar=wk_, in1=accm,
                                                                                             op0=ALU.mult, op1=ALU.add),
                          reads=[cbk % m, "pp", ak(acc_page0 + m)], writes=[ak(acc_page0 + m)])

        def mixers(l, T, blocks, sample):
            if l not in wdt_loaded:
                wdt_loaded.add(l)
                A("sp", lambda e: e.dma_start(out=wdt[:, l, :, :], in_=wi_b[l].rearrange("(c p) n -> p c n", p=128)[:, :, 3584:3592]),
                  reads=["w:wi%d:3584" % l], writes=["wdt%d" % l], dma="c4")
            rmsnorm_to_h(l, 2, T)
            bo = l * BCW
            cbk = "cb%d"; cbBk = "cB%d"
            cb_all = [cbk % m for m in range(4)]; cbB_all = [cbBk % m for m in range(4)]
            HB = 30
            A("dve", lambda e: e.tensor_copy(out=cbufB[:, :, 0:HB], in_=st_b[:, l, :, :]), reads=["st_b%d" % l], writes=cbB_all)
            sia, sla = load_wi_slot(l, 1024)
            sig, slg = load_wi_slot(l, 1536)
            bbanks = [(nextps(), nextps()) for m in range(4)]
            for kc in range(NCH):
                for m in range(4):
                    for (si_, sl_, pi_) in ((sia, sla, bbanks[m][0]), (sig, slg, bbanks[m][1])):
                        A("pe", lambda e, kc=kc, m=m, sl_=sl_, pi_=pi_: e.matmul(psb[pi_][:, 0:T], lhsT=sl_[:, kc, m * 128:(m + 1) * 128], rhs=hT[:, kc, 0:T],
                                                                              start=(kc == 0), stop=(kc == NCH - 1)),
                          reads=["slot%d" % si_, "h%d" % kc], writes=["ps%d" % pi_])
            for m in range(4):
                pa, pg2 = bbanks[m]
                tt = 16 + m % 2
                A("act", lambda e, pg2=pg2, tt=tt: e.activation(out=ar[:, tt, 0:T], in_=psb[pg2][:, 0:T], func=AF.Sigmoid), reads=["ps%d" % pg2], writes=[ak(tt)])
                A("dve", lambda e, pa=pa, tt=tt, m=m: e.tensor_tensor(out=cbufB[:, m, HB:HB + T], in0=ar[:, tt, 0:T], in1=psb[pa][:, 0:T], op=ALU.mult),
                  reads=["ps%d" % pa, ak(tt)], writes=[cbBk % m])
            A("dve", lambda e: e.tensor_copy(out=st_b[:, l, :, :], in_=cbufB[:, :, T:T + HB]), reads=cbB_all, writes=["st_b%d" % l])
            ps_reserved.add(7)
            zb = zst[:, :].bitcast(BF16)
            tbufs = [(pgb(16)[:, 0:512], ak(16)), (pgb(16)[:, 512:1024], ak(16)), (pgb(17)[:, 0:512], ak(17)), (pgb(17)[:, 512:1024], ak(17)),
                     (zb[:, 0:512], "zst"), (zb[:, 512:1024], "zst")]

            def b_conv_gen():
                pend = None
                n = 0
                for m in range(4):
                    for k in range(31):
                        tb, tk = tbufs[n % 4]
                        n += 1
                        wk_ = ppc(l, O_BCW + m * 31 + k)
                        if k % 2 == 1:
                            A("dve", lambda e, m=m, k=k, tb=tb, wk_=wk_: e.tensor_scalar(out=tb[:, 0:T], in0=cbufB[:, m, k:k + T], scalar1=wk_, scalar2=None, op0=ALU.mult),
                              reads=[cbBk % m, "pp"], writes=[tk])
                        else:
                            A("act", lambda e, m=m, k=k, tb=tb, wk_=wk_: e.activation(out=tb[:, 0:T], in_=cbufB[:, m, k:k + T], func=AF.Copy, scale=wk_),
                              reads=[cbBk % m, "pp"], writes=[tk])
                        if pend is not None:
                            pend()
                        def mm(m=m, k=k, tb=tb, tk=tk):
                            A("pe", lambda e: e.matmul(psb[7][:, 0:T], lhsT=identb[:], rhs=tb[:, 0:T], start=(k == 0), stop=(k == 30)),
                              reads=[tk, "identb"], writes=["ps7"])
                            if k == 30:
                                A("dve", lambda e: e.tensor_scalar(out=ar[:, 18 + m, 0:T], in0=psb[7][:, 0:T], scalar1=ppc(l, O_BCB + m), scalar2=None, op0=ALU.add),
                                  reads=["ps7", "pp"], writes=[ak(18 + m)])
                        pend = mm
                        yield
                pend()
                yield
            bg["gen"] = b_conv_gen()
            si, sl = load_wi_slot(l, 0)
            for m in range(4):
                pi = proj_fm(si, sl, m, T)
                A("act", lambda e, pi=pi, m=m: e.activation(out=ar[:, m, 0:T], in_=psb[pi][:, 0:T], func=AF.Gelu), reads=["ps%d" % pi], writes=[ak(m)])
            si, sl = load_wi_slot(l, 512)

            def a_vproj(b, c0, L):
                pi = nextps()
                for kc in range(NCH):
                    A("pe", lambda e, kc=kc, pi=pi: e.matmul(psb[pi][0:L, :], lhsT=hT[:, kc, c0:c0 + L], rhs=sl[:, kc, 0:512],
                                                             start=(kc == 0), stop=(kc == NCH - 1)),
                      reads=["slot%d" % si, "h%d" % kc], writes=["ps%d" % pi])
                vp = 4 + b % 2
                A("act", lambda e: e.activation(out=ar[0:L, vp, :], in_=psb[pi][0:L, :], func=AF.Gelu), reads=["ps%d" % pi], writes=[ak(vp)])
                A("dve", lambda e: e.bn_stats(out=bnst[0:L, 0:6], in_=ar[0:L, vp, :]), reads=[ak(vp)], writes=["bnst"])
                A("dve", lambda e: e.bn_aggr(out=bnst[0:L, 6:8], in_=bnst[0:L, 0:6]), reads=["bnst"], writes=["bnst"])
                A("dve", lambda e: e.tensor_scalar(out=bnst[0:L, 7:8], in0=bnst[0:L, 7:8], scalar1=EPS, scalar2=None, op0=ALU.add), reads=["bnst"], writes=["bnst"])
                A("act", lambda e: e.activation(out=bnst[0:L, 7:8], in_=bnst[0:L, 7:8], func=AF.Sqrt), reads=["bnst"], writes=["bnst"])
                A("dve", lambda e: e.reciprocal(out=bnst[0:L, 7:8], in_=bnst[0:L, 7:8]), reads=["bnst"], writes=["bnst"])
                A("dve", lambda e: e.tensor_scalar(out=ar[0:L, vp, :], in0=ar[0:L, vp, :], scalar1=bnst[0:L, 6:7], scalar2=bnst[0:L, 7:8],
                                                   op0=ALU.subtract, op1=ALU.mult), reads=["bnst", ak(vp)], writes=[ak(vp)])
                A("dve", lambda e: e.tensor_tensor(out=ar[0:L, vp, :], in0=ar[0:L, vp, :], in1=bc[0:L, bo:bo + 512], op=ALU.mult),
                  reads=["bc", ak(vp)], writes=[ak(vp)])
                A("dve", lambda e: e.tensor_tensor(out=ar[0:L, vp, :], in0=ar[0:L, vp, :], in1=bc[0:L, bo + 512:bo + 1024], op=ALU.add),
                  reads=["bc", ak(vp)], writes=[ak(vp)])
                if sample:
                    A("pool", lambda e: e.dma_start(out=s_v[l, :, :], in_=ar[0:L, vp, :]), reads=[ak(vp)], dma="o_v")
                vb = pgb(6)[:, (b % 2) * 512:(b % 2) * 512 + 512]
                A("act", lambda e: e.activation(out=vb[0:L, :], in_=ar[0:L, vp, :], func=AF.Copy), reads=[ak(vp)], writes=["vb%d" % (b % 2)])

            def a_spatial(b, c0, L):
                vb = pgb(6)[:, (b % 2) * 512:(b % 2) * 512 + 512]
                vbk = "vb%d" % (b % 2)
                pi = nextps()
                for h in range(4):
                    o_ = psb[pi][:, h * 128:h * 128 + L]
                    A("pe", lambda e, o_=o_, h=h: e.matmul(o_, lhsT=vb[0:L, h * 128:(h + 1) * 128], rhs=wsTb[0:L, l, h, 0:L], start=True, stop=False),
                      reads=[vbk, "wsTb"], writes=["ps%d" % pi])
                    A("pe", lambda e, o_=o_, h=h: e.matmul(o_, lhsT=onesb[0:1, :], rhs=bsh[0:1, l * 512 + h * 128: l * 512 + h * 128 + L], start=False, stop=False),
                      reads=["bsh", "onesb"], writes=["ps%d" % pi])
                    A("pe", lambda e, o_=o_, h=h: e.matmul(o_, lhsT=onesb[0:1, :], rhs=bsl[0:1, l * 512 + h * 128: l * 512 + h * 128 + L], start=False, stop=True),
                      reads=["bsl", "onesb"], writes=["ps%d" % pi])
                for h in range(4):
                    A("dve", lambda e, h=h: e.tensor_tensor(out=mix_ap(h, c0, c0 + L), in0=ar[:, h, c0:c0 + L], in1=psb[pi][:, h * 128:h * 128 + L], op=ALU.mult),
                      reads=["ps%d" % pi, ak(h)], writes=[mk_(h)])

            for b, (c0, L) in enumerate(blocks):
                a_vproj(b, c0, L)
                if b > 0:
                    a_spatial(b - 1, *blocks[b - 1])
            a_spatial(len(blocks) - 1, *blocks[-1])
            HD = 2
            A("dve", lambda e: e.tensor_copy(out=cbuf[:, :, 0:HD], in_=st_d[:, l, :, :]), reads=["st_d%d" % l], writes=cb_all)
            sic, slc = load_wi_slot(l, 4104)
            six, slx = load_wi_slot(l, 4616)
            for m in range(4):
                pc = proj_fm(sic, slc, m, T)
                px = proj_fm(six, slx, m, T)
                tt = 8 + m % 2
                A("dve", lambda e, pc=pc, tt=tt: e.tensor_copy(out=ar[:, tt, 0:T], in_=psb[pc][:, 0:T]), reads=["ps%d" % pc], writes=[ak(tt)])
                A("dve", lambda e, px=px, tt=tt, m=m: e.tensor_tensor(out=cbuf[:, m, HD:HD + T], in0=ar[:, tt, 0:T], in1=psb[px][:, 0:T], op=ALU.mult),
                  reads=["ps%d" % px, ak(tt)], writes=[cbk % m])
            conv_taps("dve", cbuf, cbk, l, 3, O_DCW, 4, T, 10)
            A("dve", lambda e: e.tensor_copy(out=st_d[:, l, :, :], in_=cbuf[:, :, T:T + HD]), reads=cb_all, writes=["st_d%d" % l])
            sib, slb = load_wi_slot(l, 3592)
            for m in range(4):
                pi = proj_fm(sib, slb, m, T)
                A("dve", lambda e, pi=pi, m=m: e.tensor_tensor(out=mix_ap(12 + m, 0, T), in0=ar[:, 10 + m, 0:T], in1=psb[pi][:, 0:T], op=ALU.mult),
                  reads=[ak(10 + m), "ps%d" % pi], writes=[mk_(12 + m)])
            HC = 3
            for part in range(2):
                sip, slp = load_wi_slot(l, 2560 + part * 512)
                A("dve", lambda e, part=part: e.tensor_copy(out=cbuf[:, :, 0:HC], in_=st_c[:, l, 4 * part:4 * part + 4, :]), reads=["st_c%d" % l], writes=cb_all)
                for m in range(4):
                    pi = proj_fm(sip, slp, m, T)
                    A("dve", lambda e, pi=pi, m=m: e.tensor_copy(out=cbuf[:, m, HC:HC + T], in_=psb[pi][:, 0:T]), reads=["ps%d" % pi], writes=[cbk % m])
                conv_taps("dve", cbuf, cbk, l, 4, O_CCW, 4, T, 6, boff=O_CCB, chunk0=4 * part)
                A("dve", lambda e, part=part: e.tensor_copy(out=st_c[:, l, 4 * part:4 * part + 4, :], in_=cbuf[:, :, T:T + HC]), reads=cb_all, writes=["st_c%d" % l])
                for m in range(4):
                    if part == 0:
                        A("act", lambda e, m=m: e.activation(out=ar[:, m, 0:T], in_=ar[:, 6 + m, 0:T], func=AF.Silu), reads=[ak(6 + m)], writes=[ak(m)])
                    else:
                        A("act", lambda e, m=m: e.activation(out=pgb(4 + m // 2)[:, (m % 2) * 512:(m % 2) * 512 + T], in_=ar[:, 6 + m, 0:T], func=AF.Silu),
                          reads=[ak(6 + m)], writes=[ak(4 + m // 2)])

            def bct(m, c0, L):
                return pgb(4 + m // 2)[:, (m % 2) * 512 + c0:(m % 2) * 512 + c0 + L]

            xdtz = pgb(14).rearrange("p (h n) -> p h n", h=8)
            hz = hzt[:, :].rearrange("p (h n) -> p h n", h=8)
            A("dve", lambda e: e.memset(pgb(14), 0.0), writes=[ak(14)])
            H4 = st_h[:, l, :].rearrange("p (c two q) -> p c two q", c=4, two=2)
            hz4 = hzt[:, :].rearrange("p (c two n) -> p c two n", c=4, two=2)

            def refresh_hz():
                A("act", lambda e: e.activation(out=hz4[:, :, 0, 0:64], in_=H4[:, :, 0, :], func=AF.Copy), reads=["st_h%d" % l], writes=["hz"])
                A("act", lambda e: e.activation(out=hz4[:, :, 1, 64:128], in_=H4[:, :, 1, :], func=AF.Copy), reads=["st_h%d" % l], writes=["hz"])
            refresh_hz()
            Abc = bc[:, bo + 1032:bo + 1040]; dtb = bc[:, bo + 1024:bo + 1032]
            NB = len(blocks); L0 = blocks[0][1]; W8 = NB * 8
            smv = sm[:, :].rearrange("p (i n) -> p i n", i=6)
            dtv_all = smv[:, 0, 0:W8]; dtA_all = smv[:, 1, 0:W8]; acum_all = smv[:, 2, 0:W8]
            pdt = nextps()
            for b, (c0, L) in enumerate(blocks):
                for kc in range(NCH):
                    A("pe", lambda e, kc=kc, b=b, c0=c0, L=L: e.matmul(psb[pdt][0:L, b * 8:(b + 1) * 8], lhsT=hT[:, kc, c0:c0 + L], rhs=wdt[:, l, kc, :], start=(kc == 0), stop=(kc == NCH - 1)),
                      reads=["wdt%d" % l, "h%d" % kc], writes=["ps%d" % pdt])
            A("dve", lambda e: e.tensor_tensor(out=dtv_all[0:L0, :].rearrange("p (b n) -> p b n", b=NB), in0=psb[pdt][0:L0, 0:W8].rearrange("p (b n) -> p b n", b=NB),
                                               in1=dtb[0:L0, :].unsqueeze(1).broadcast_to([L0, NB, 8]), op=ALU.add), reads=["ps%d" % pdt, "bc"], writes=["smA"])
            A("act", lambda e: e.activation(out=dtv_all[0:L0, :], in_=dtv_all[0:L0, :], func=AF.Exp), reads=["smA"], writes=["smA"])
            A("act", lambda e: e.activation(out=dtv_all[0:L0, :], in_=dtv_all[0:L0, :], func=AF.Ln, bias=1.0), reads=["smA"], writes=["smA"])
            A("dve", lambda e: e.tensor_tensor(out=dtA_all[0:L0, :].rearrange("p (b n) -> p b n", b=NB), in0=dtv_all[0:L0, :].rearrange("p (b n) -> p b n", b=NB),
                                               in1=Abc[0:L0, :].unsqueeze(1).broadcast_to([L0, NB, 8]), op=ALU.mult), reads=["smA", "bc"], writes=["smA"])
            pcs = nextps()
            A("pe", lambda e: e.matmul(psb[pcs][0:L0, 0:W8], lhsT=tri[0:L0, 0:L0], rhs=dtA_all[0:L0, :], start=True, stop=True), reads=["smA", "tri"], writes=["ps%d" % pcs])
            A("dve", lambda e: e.tensor_copy(out=acum_all[0:L0, :], in_=psb[pcs][0:L0, 0:W8]), reads=["ps%d" % pcs], writes=["smA"])

            def ssd_block(b, c0, L):
                dtv = smv[:, 0, b * 8:(b + 1) * 8]; dtA = smv[:, 1, b * 8:(b + 1) * 8]; acum = smv[:, 2, b * 8:(b + 1) * 8]
                wdec = smv[:, 3, b * 8:(b + 1) * 8]; dtw = smv[:, 4, b * 8:(b + 1) * 8]; edec = smv[:, 5, b * 8:(b + 1) * 8]
                smk = "smb%d" % b
                rep = ar[:, 10:12, :].rearrange("p a (h m) -> p (a h) m", h=4)
                A("dve", lambda e, dtA=dtA: e.tensor_copy(out=rep[0:L, :, :], in_=dtA[0:L, :].unsqueeze(2).broadcast_to([L, 8, 128])), reads=["smA"], writes=aks(10, 12))
                pb0, pb1 = nextps(), nextps()
                for h in range(8):
                    pbi = pb0 if h < 4 else pb1
                    A("pe", lambda e, pbi=pbi, h=h: e.matmul(psb[pbi][:, (h % 4) * 128:(h % 4) * 128 + L], lhsT=rep[0:L, h, :], rhs=tri[0:L, 0:L], start=True, stop=True),
                      reads=aks(10, 12) + ["tri"], writes=["ps%d" % pbi])

                def bcv(pbi, L=L):
                    return psb[pbi][:, :].rearrange("p (h t) -> p h t", h=4)[:, :, 0:L]
                Cs = pgb(13).rearrange("p (h t) -> p h t", h=8)
                for half, pbi in enumerate((pb0, pb1)):
                    A("act", lambda e, half=half, pbi=pbi: e.activation(out=Cs[:, 4 * half:4 * half + 4, 0:L], in_=bcv(pbi), func=AF.Exp),
                      reads=["ps%d" % pbi], writes=[ak(13)])
                    A("act", lambda e, half=half, pbi=pbi, edec=edec: e.activation(out=edec[:, 4 * half:4 * half + 4], in_=bcv(pbi)[:, :, L - 1], func=AF.Exp),
                      reads=["ps%d" % pbi], writes=[smk])
                    A("dve", lambda e, half=half, pbi=pbi, wdec=wdec, acum=acum: e.tensor_tensor(out=wdec[0:L, 4 * half:4 * half + 4], in0=bcv(pbi)[0:L, :, L - 1], in1=acum[0:L, 4 * half:4 * half + 4], op=ALU.subtract),
                      reads=["ps%d" % pbi, smk, "smA"], writes=[smk])
                A("act", lambda e, wdec=wdec: e.activation(out=wdec[0:L, :], in_=wdec[0:L, :], func=AF.Exp), reads=[smk], writes=[smk])
                A("dve", lambda e, wdec=wdec, dtw=dtw, dtv=dtv: e.tensor_tensor(out=dtw[0:L, :], in0=wdec[0:L, :], in1=dtv[0:L, :], op=ALU.mult), reads=[smk, "smA"], writes=[smk])
                d1 = ar[:, 10:12, :].rearrange("p a (h t) -> p (a h) t", h=4)
                for half, pbi in enumerate((pb0, pb1)):
                    A("dve", lambda e, half=half, pbi=pbi: e.tensor_tensor(out=d1[0:L, 4 * half:4 * half + 4, 0:L], in0=bcv(pbi)[0:L], in1=maskneg[0:L, 0:L].unsqueeze(1).broadcast_to([L, 4, L]), op=ALU.add),
                      reads=["ps%d" % pbi, "maskneg"], writes=aks(10, 12))
                A("dve", lambda e, acum=acum: e.tensor_tensor(out=d1[0:L, :, 0:L], in0=d1[0:L, :, 0:L], in1=acum[0:L, :].unsqueeze(2).broadcast_to([L, 8, L]), op=ALU.subtract),
                  reads=aks(10, 12) + ["smA"], writes=aks(10, 12))
                Mb = pgb(12).rearrange("p (h t) -> p h t", h=8)
                A("act", lambda e: e.activation(out=Mb[0:L, :, 0:L], in_=d1[0:L, :, 0:L], func=AF.Exp), reads=aks(10, 12), writes=[ak(12)])
                pG = nextps()
                for g in range(2):
                    A("pe", lambda e, g=g, pG=pG: e.matmul(psb[pG][0:L, g * 128:g * 128 + L], lhsT=bct(g, c0, L), rhs=bct(2 + g, c0, L), start=True, stop=True),
                      reads=aks(4, 6), writes=["ps%d" % pG])
                for g in range(2):
                    A("dve", lambda e, g=g, pG=pG: e.tensor_tensor(out=Mb[0:L, 4 * g:4 * g + 4, 0:L], in0=Mb[0:L, 4 * g:4 * g + 4, 0:L],
                                                                   in1=psb[pG][0:L, g * 128:g * 128 + L].unsqueeze(1).broadcast_to([L, 4, L]), op=ALU.mult),
                      reads=[ak(12), "ps%d" % pG], writes=[ak(12)])
                for g in range(2):
                    A("dve", lambda e, g=g: e.tensor_tensor(out=Cs[:, 4 * g:4 * g + 4, 0:L], in0=Cs[:, 4 * g:4 * g + 4, 0:L],
                                                            in1=bct(2 + g, c0, L).unsqueeze(1).broadcast_to([128, 4, L]), op=ALU.mult),
                      reads=[ak(13)] + aks(4, 6), writes=[ak(13)])
                px = nextps()
                for m in range(4):
                    A("pe", lambda e, m=m, px=px: e.transpose(psb[px][0:L, m * 128:(m + 1) * 128], ar[:, m, c0:c0 + L], identf[:]),
                      reads=[ak(m), "identf"], writes=["ps%d" % px])
                pxv = psb[px][:, :].rearrange("p (c two q) -> p c two q", c=4, two=2)
                xz4 = pgb(14).rearrange("p (c two n) -> p c two n", c=4, two=2)
                dt4 = dtv.rearrange("p (c two) -> p c two", two=2)
                for two in range(2):
                    A("dve", lambda e, two=two, pxv=pxv, xz4=xz4, dt4=dt4: e.tensor_tensor(out=xz4[0:L, :, two, two * 64:two * 64 + 64], in0=pxv[0:L, :, two, :],
                                                                                     in1=dt4[0:L, :, two].unsqueeze(2).broadcast_to([L, 4, 64]), op=ALU.mult),
                      reads=["ps%d" % px, "smA"], writes=[ak(14)])
                xdte = pgb(15)[:, 0:512]
                A("dve", lambda e, px=px, xdte=xdte, dtw=dtw: e.tensor_tensor(out=xdte[0:L, :].rearrange("p (h q) -> p h q", h=8), in0=psb[px][0:L, :].rearrange("p (h q) -> p h q", h=8),
                                                                     in1=dtw[0:L, :].unsqueeze(2).broadcast_to([L, 8, 64]), op=ALU.mult),
                  reads=["ps%d" % px, smk], writes=[ak(15)])
                pB = nextps()
                pBb = psb[pB][:, :].bitcast(BF16)
                for g in range(2):
                    A("pe", lambda e, g=g, pBb=pBb: e.transpose(pBb[0:L, g * 128:(g + 1) * 128], bct(g, c0, L), identb[:]),
                      reads=aks(4, 6) + ["identb"], writes=["ps%d" % pB])
                Btm = pgb(15)[:, 512:768]
                A("act", lambda e, pBb=pBb, Btm=Btm: e.activation(out=Btm[0:L, :], in_=pBb[0:L, 0:256], func=AF.Copy), reads=["ps%d" % pB], writes=[ak(15)])
                py = nextps()
                for c in range(4):
                    o_ = psb[py][:, c * 128:c * 128 + L]
                    for two in range(2):
                        h = 2 * c + two
                        A("pe", lambda e, o_=o_, h=h, two=two: e.matmul(o_, lhsT=xdtz[0:L, h, :], rhs=Mb[0:L, h, 0:L], start=(two == 0), stop=False),
                          reads=[ak(14), ak(12)], writes=["ps%d" % py])
                    for two in range(2):
                        h = 2 * c + two
                        A("pe", lambda e, o_=o_, h=h, two=two: e.matmul(o_, lhsT=hz[:, h, :], rhs=Cs[:, h, 0:L], start=False, stop=(two == 1)),
                          reads=["hz", ak(13)], writes=["ps%d" % py])
                for c in range(4):
                    A("dve", lambda e, c=c, py=py: e.scalar_tensor_tensor(out=ar[:, 6 + c, c0:c0 + L], in0=ar[:, c, c0:c0 + L], scalar=ppc(l, O_CDSK + c),
                                                                          in1=psb[py][:, c * 128:c * 128 + L], op0=ALU.mult, op1=ALU.add),
                      reads=[ak(c), "pp", "ps%d" % py], writes=[ak(6 + c)])
                pS = nextps()
                for g in range(2):
                    A("pe", lambda e, g=g, pS=pS, Btm=Btm, xdte=xdte: e.matmul(psb[pS][:, g * 256:(g + 1) * 256], lhsT=Btm[0:L, g * 128:(g + 1) * 128], rhs=xdte[0:L, g * 256:(g + 1) * 256], start=True, stop=True),
                      reads=[ak(15)], writes=["ps%d" % pS])
                Hv = st_h[:, l, :].rearrange("p (h q) -> p h q", h=8)
                A("dve", lambda e, Hv=Hv, edec=edec: e.tensor_tensor(out=Hv, in0=Hv, in1=edec.unsqueeze(2).broadcast_to([128, 8, 64]), op=ALU.mult), reads=[smk, "st_h%d" % l], writes=["st_h%d" % l])
                A("dve", lambda e, pS=pS: e.tensor_tensor(out=st_h[:, l, :], in0=st_h[:, l, :], in1=psb[pS][:, :], op=ALU.add), reads=["ps%d" % pS, "st_h%d" % l], writes=["st_h%d" % l])
                refresh_hz()
            for b, (c0, L) in enumerate(blocks):
                ssd_block(b, c0, L)
            siz, slz = load_wi_slot(l, 2048)
            for c in range(4):
                pi = proj_fm(siz, slz, c, T)
                A("act", lambda e, pi=pi: e.activation(out=zst[:, 0:T], in_=psb[pi][:, 0:T], func=AF.Silu), reads=["ps%d" % pi], writes=["zst"])
                A("dve", lambda e, c=c: e.tensor_tensor(out=ar[:, 6 + c, 0:T], in0=ar[:, 6 + c, 0:T], in1=zst[:, 0:T], op=ALU.mult), reads=["zst", ak(6 + c)], writes=[ak(6 + c)])
                A("act", lambda e, c=c: e.activation(out=pgb(c // 2)[:, (c % 2) * 512:(c % 2) * 512 + T], in_=ar[:, 6 + c, 0:T], func=AF.Square), reads=[ak(6 + c)], writes=[ak(c // 2)])
            for g in range(2):
                pi = nextps()
                for cc in range(2):
                    c = 2 * g + cc
                    A("pe", lambda e, c=c, cc=cc, pi=pi: e.matmul(psb[pi][:, 0:T], lhsT=onesb[:], rhs=pgb(c // 2)[:, (c % 2) * 512:(c % 2) * 512 + T], start=(cc == 0), stop=(cc == 1)),
                      reads=[ak(c // 2), "onesb"], writes=["ps%d" % pi])
                rstd_from_ps(pi, T, rs[:, g, 0:T], EPS * 256)
                for cc in range(2):
                    c = 2 * g + cc
                    A("dve", lambda e, c=c, g=g: e.scalar_tensor_tensor(out=mix_ap(8 + c, 0, T), in0=ar[:, 6 + c, 0:T], scalar=ngc[:, l, c:c + 1], in1=rs[:, g, 0:T], op0=ALU.mult, op1=ALU.mult),
                      reads=[ak(6 + c), "rs", "ngc"], writes=[mk_(8 + c)])
            bg_flush()
            ps_reserved.discard(7)
            p1, p2 = nextps(), nextps()
            for m in range(4):
                A("pe", lambda e, m=m: e.matmul(psb[p1][:, 0:T], lhsT=onesf[:], rhs=ar[:, 18 + m, 0:T], start=(m == 0), stop=(m == 3)),
                  reads=[ak(18 + m), "onesf"], writes=["ps%d" % p1])
            for m in range(4):
                tt = 16 + m % 2
                A("act", lambda e, m=m, tt=tt: e.activation(out=ar[:, tt, 0:T], in_=ar[:, 18 + m, 0:T], func=AF.Square), reads=[ak(18 + m)], writes=[ak(tt)])
                A("pe", lambda e, m=m, tt=tt: e.matmul(psb[p2][:, 0:T], lhsT=onesf[:], rhs=ar[:, tt, 0:T], start=(m == 0), stop=(m == 3)),
                  reads=[ak(tt), "onesf"], writes=["ps%d" % p2])
            mean = rs[:, 0, 0:T]; rstd = rs[:, 1, 0:T]
            A("dve", lambda e: e.tensor_scalar(out=mean, in0=psb[p1][:, 0:T], scalar1=1.0 / 512, scalar2=None, op0=ALU.mult), reads=["ps%d" % p1], writes=["rs"])
            A("dve", lambda e: e.tensor_tensor(out=rstd, in0=mean, in1=mean, op=ALU.mult), reads=["rs"], writes=["rs"])
            A("dve", lambda e: e.scalar_tensor_tensor(out=rstd, in0=psb[p2][:, 0:T], scalar=1.0 / 512, in1=rstd, op0=ALU.mult, op1=ALU.subtract),
              reads=["rs", "ps%d" % p2], writes=["rs"])
            A("dve", lambda e: e.tensor_scalar(out=rstd, in0=rstd, scalar1=EPS, scalar2=None, op0=ALU.add), reads=["rs"], writes=["rs"])
            A("act", lambda e: e.activation(out=rstd, in_=rstd, func=AF.Sqrt), reads=["rs"], writes=["rs"])
            A("dve", lambda e: e.reciprocal(out=rstd, in_=rstd), reads=["rs"], writes=["rs"])
            for m in range(4):
                A("dve", lambda e, m=m: e.tensor_tensor(out=ar[:, 18 + m, 0:T], in0=ar[:, 18 + m, 0:T], in1=mean, op=ALU.subtract), reads=["rs", ak(18 + m)], writes=[ak(18 + m)])
                A("dve", lambda e, m=m: e.tensor_tensor(out=ar[:, 18 + m, 0:T], in0=ar[:, 18 + m, 0:T], in1=rstd, op=ALU.mult), reads=["rs", ak(18 + m)], writes=[ak(18 + m)])
                A("act", lambda e, m=m: e.activation(out=mix_ap(4 + m, 0, T), in_=ar[:, 18 + m, 0:T], func=AF.Silu, bias=ppc(l, O_BLB + m), scale=ppc(l, O_BLG + m)),
                  reads=[ak(18 + m), "pp"], writes=[mk_(4 + m)])
            proj_postnorm(l, 3, T, wo_b[l], "wo%d" % l, NCH, lambda kc: mix_ap(kc, 0, T), lambda kc: [mk_(kc)], 0, 8, s_order=[0, 2, 3, 1])

        ngc = sb("ngc", [128, 2, 4])
        for l in range(2):
            A("dve", lambda e, l=l: e.tensor_scalar(out=ngc[:, l, :], in0=pp[:, l * NPP + O_CNG: l * NPP + O_CNG + 4], scalar1=16.0, scalar2=None, op0=ALU.mult),
              reads=["pp"], writes=["ngc"])

        def run_tile(T, blocks, sample, x_src, y_dst):
            xv = x_src.rearrange("(c p) t -> p c t", p=128)
            for q in range(4):
                A("sp", lambda e, q=q: e.dma_start(out=xT[:, 4 * q:4 * q + 4, 0:T], in_=xv[:, 4 * q:4 * q + 4, :]), writes=["x%d" % c for c in range(4 * q, 4 * q + 4)], dma="xin%d" % q)
            for l in range(2):
                ffn(l, 0, T)
                mixers(l, T, blocks, sample)
                ffn(l, 1, T)
            yv = y_dst.rearrange("(c p) t -> p c t", p=128)
            for q in range(4):
                A("pool", lambda e, q=q: e.dma_start(out=yv[:, 4 * q:4 * q + 4, :], in_=xT[:, 4 * q:4 * q + 4, 0:T]), reads=["x%d" % c for c in range(4 * q, 4 * q + 4)], dma="yout%d" % q)

        def store_states(tag):
            for l in range(2):
                A("pool", lambda e, l=l: e.dma_start(out=o_cb[tag][l].rearrange("(c p) k -> p c k", p=128), in_=st_b[:, l, :, :]), reads=["st_b%d" % l], dma="os0")
                A("pool", lambda e, l=l: e.dma_start(out=o_cc[tag][l].rearrange("(c p) k -> p c k", p=128), in_=st_c[:, l, :, :]), reads=["st_c%d" % l], dma="os1")
                A("pool", lambda e, l=l: e.dma_start(out=o_cd[tag][l].rearrange("(c p) k -> p c k", p=128), in_=st_d[:, l, :, :]), reads=["st_d%d" % l], dma="os2")
                A("pool", lambda e, l=l: e.dma_start(out=o_h[tag][l], in_=st_h[:, l, :]), reads=["st_h%d" % l], dma="os3")

        if NP > 0:
            for l in range(2):
                A("dve", lambda e, l=l: e.memset(st_b[:, l, :, :], 0.0), writes=["st_b%d" % l])
                A("dve", lambda e, l=l: e.memset(st_c[:, l, :, :], 0.0), writes=["st_c%d" % l])
                A("dve", lambda e, l=l: e.memset(st_d[:, l, :, :], 0.0), writes=["st_d%d" % l])
                A("dve", lambda e, l=l: e.memset(st_h[:, l, :], 0.0), writes=["st_h%d" % l])
            blocks = [(i * 128, 128) for i in range(4)]
            for i in range(NP):
                run_tile(TP, blocks, False, xT_d[:, i * TP:(i + 1) * TP], yT_d[:, i * TP:(i + 1) * TP])
            store_states("p")
        for l in range(2):
            A("sp", lambda e, l=l: e.dma_start(out=st_b[:, l, :, :], in_=cb_in[l].rearrange("(c p) k -> p c k", p=128)), writes=["st_b%d" % l], dma="is0")
            A("sp", lambda e, l=l: e.dma_start(out=st_c[:, l, :, :], in_=cc_in[l].rearrange("(c p) k -> p c k", p=128)), writes=["st_c%d" % l], dma="is1")
            A("sp", lambda e, l=l: e.dma_start(out=st_d[:, l, :, :], in_=cd_in[l].rearrange("(c p) k -> p c k", p=128)), writes=["st_d%d" % l], dma="is2")
            A("sp", lambda e, l=l: e.dma_start(out=st_h[:, l, :], in_=h_in[l]), writes=["st_h%d" % l], dma="is3")
        run_tile(16, [(0, 16)], True, xsT_d, ysT_d)
        store_states("s")
        S.emit(nc, es)
    return nc


def _prep_shared(inp):
    f = np.float32
    wfi = np.ascontiguousarray(inp["ffn_w_in"].reshape(2, 2, D, 2, NHC, 128).transpose(0, 1, 2, 4, 3, 5).reshape(2, 2, D, 2 * DFF), dtype=f)
    pp = np.zeros((128, 2, NPP), f)
    for l in range(2):
        pp[:, l, O_NG:O_NG + 96] = inp["norm_g"][l].reshape(6, 16, 128).transpose(2, 0, 1).reshape(128, 96)
        pp[:, l, O_BCW:O_BCW + 124] = inp["b_conv_w"][l].reshape(31, 4, 128).transpose(2, 1, 0).reshape(128, 124)
        pp[:, l, O_BCB:O_BCB + 4] = inp["b_conv_b"][l].reshape(4, 128).T
        pp[:, l, O_BLG:O_BLG + 4] = inp["b_ln_g"][l].reshape(4, 128).T
        pp[:, l, O_BLB:O_BLB + 4] = inp["b_ln_b"][l].reshape(4, 128).T
        pp[:, l, O_CCW:O_CCW + 32] = inp["c_conv_w"][l].reshape(4, 8, 128).transpose(2, 1, 0).reshape(128, 32)
        pp[:, l, O_CCB:O_CCB + 8] = inp["c_conv_b"][l].reshape(8, 128).T
        pp[:, l, O_CDSK:O_CDSK + 4] = np.repeat(inp["c_d"][l], 64).reshape(4, 128).T
        pp[:, l, O_CNG:O_CNG + 4] = inp["c_norm_g"][l].reshape(4, 128).T
        pp[:, l, O_DCW:O_DCW + 12] = inp["d_conv_w"][l].reshape(3, 4, 128).transpose(2, 1, 0).reshape(128, 12)
    bcr = np.zeros((1, 2, BCW), f)
    for l in range(2):
        bcr[0, l, 0:512] = inp["a_ln_g"][l]; bcr[0, l, 512:1024] = inp["a_ln_b"][l]
        bcr[0, l, 1024:1032] = inp["c_dt_bias"][l]; bcr[0, l, 1032:1040] = inp["c_a_log"][l]
    wsT = np.ascontiguousarray(inp["a_ws"].transpose(3, 0, 1, 2).reshape(128, 1024), dtype=f)
    bsr = np.ascontiguousarray(inp["a_bs"].reshape(1, 1024), dtype=f)
    return {"wfi": wfi, "wfo": np.ascontiguousarray(inp["ffn_w_out"], dtype=f), "wi": np.ascontiguousarray(inp["w_in"], dtype=f),
            "wo": np.ascontiguousarray(inp["w_out"], dtype=f), "pp": pp.reshape(128, 2 * NPP), "bcr": bcr.reshape(1, 2 * BCW),
            "wsT": wsT, "bsr": bsr}


def run(inp, NP, n_cores, n_prompt):
    f = np.float32
    shared = _prep_shared(inp)
    nc = build(NP)
    in_maps = []
    NTOK = max(NP, 1) * TP
    pcores = [0, 1, 4, 5][:n_prompt] if n_cores == 8 else list(range(n_prompt))
    for c in range(n_cores):
        m = dict(shared)
        if c in pcores and NP > 0:
            m["xT"] = np.ascontiguousarray(inp["x_prompt"][pcores.index(c)].T, dtype=f)
        else:
            m["xT"] = np.zeros((D, NTOK), f)
        m["xsT"] = np.ascontiguousarray(inp["x_sample"][c].T, dtype=f)
        m["cb_in"] = np.ascontiguousarray(inp["cache_conv_b"][:, c].transpose(0, 2, 1), dtype=f)
        m["cc_in"] = np.ascontiguousarray(inp["cache_conv_c"][:, c].transpose(0, 2, 1), dtype=f)
        m["cd_in"] = np.ascontiguousarray(inp["cache_conv_d"][:, c].transpose(0, 2, 1), dtype=f)
        m["h_in"] = np.ascontiguousarray(inp["state_ssm"][:, c].reshape(2, 512, 128).transpose(0, 2, 1), dtype=f)
        in_maps.append(m)
    res = run_bass_kernel_spmd(nc, in_maps, core_ids=list(range(n_cores)))
    R = res.results

    def st(tag, cores):
        cb = np.stack([R[c][tag + "_cb"].transpose(0, 2, 1) for c in cores], 1)
        cc = np.stack([R[c][tag + "_cc"].transpose(0, 2, 1) for c in cores], 1)
        hh = np.stack([R[c][tag + "_h"].transpose(0, 2, 1).reshape(2, 8, 64, 128) for c in cores], 1)
        cd = np.stack([R[c][tag + "_cd"].transpose(0, 2, 1) for c in cores], 1)
        return [np.ascontiguousarray(a, dtype=f) for a in (cb, cc, hh, cd)]

    pc = pcores; sc = list(range(n_cores))
    y_prompt = np.ascontiguousarray(np.stack([R[c]["yT"].T for c in pc], 0), dtype=f) if NP > 0 else None
    y_sample = np.ascontiguousarray(np.stack([R[c]["ysT"].T for c in sc], 0), dtype=f)
    p = st("p", pc) if NP > 0 else [None] * 4
    s = st("s", sc)
    s_v = np.ascontiguousarray(np.stack([R[c]["s_v"] for c in sc], 1), dtype=f)
    return (y_prompt, y_sample, p[0], p[1], p[2], p[3], s_v, s[0], s[1], s[2], s[3])


def kernel(**inputs):
    return run(inputs, 16, 8, 4)
```
